# Optimizing a Trainium2 kernel written in Bass

```python
import math
import jax
import jax.numpy as jnp
from jax import lax
import numpy as np

D_MODEL = 2048
BATCH = 2
SEQ = 4096
DEPTH = 4

GRID_W = 64
CTX_LEN = 256
NORM_EPS = 1e-6
N_BRANCH = 3

ATT_HEADS = 8
ATT_KV_HEADS = 2
HEAD_DIM = 128
ATT_W = ATT_HEADS * HEAD_DIM
ATT_KV_W = ATT_KV_HEADS * HEAD_DIM
AXIS_ROPE_DIM = HEAD_DIM // 2
ROPE_THETA = 10000.0
Q_BLOCK = 128

HY_W = 1024
HY_ORDER = 2
HY_SHORT = 3
HY_BANDS = 16
HY_EMB = 1 + 2 * HY_BANDS
HY_FILTER_HIDDEN = 64
HY_DECAY_TARGET = 1e-2
HY_DECAY_FAST = 0.3
HY_DECAY_SLOW = 1.5

DN_QK_HEADS = 4
DN_V_HEADS = 8
DN_HEAD_DIM = 128
DN_QK_W = DN_QK_HEADS * DN_HEAD_DIM
DN_V_W = DN_V_HEADS * DN_HEAD_DIM
DN_SHORT = 3
DN_CHUNK = 64

IN_WIDTHS = (ATT_W, ATT_KV_W, ATT_KV_W, ATT_W,
             HY_W, HY_W, HY_W, HY_W,
             DN_QK_W, DN_QK_W, DN_V_W, DN_V_W, 2 * DN_V_HEADS, 2 * DN_V_HEADS,
             N_BRANCH * D_MODEL)
IN_W = sum(IN_WIDTHS)

kernel_name = 'hybrid_gqa_hyena_deltanet_dit'


def rms_norm(x, g):
    xf = x.astype(jnp.float32)
    y = xf * lax.rsqrt(jnp.mean(xf * xf, axis=-1, keepdims=True) + NORM_EPS)
    return (y * g.astype(jnp.float32)).astype(x.dtype)


def l2_normalize(x):
    xf = x.astype(jnp.float32)
    return xf * lax.rsqrt(jnp.sum(xf * xf, axis=-1, keepdims=True) + 1e-6)


def centred_depthwise_conv(u, w):
    pad = w.shape[0] // 2
    return lax.conv_general_dilated(u, w[:, None, :].astype(u.dtype), window_strides=(1,),
                                    padding=[(pad, pad)], dimension_numbers=('NWC', 'WIO', 'NWC'),
                                    feature_group_count=u.shape[-1])


def ada_modulation(cond, w_mod, b_mod):
    m = jax.nn.silu(cond) @ w_mod + b_mod
    return jnp.split(m, 3, axis=-1)


def rotate_half_axis(xa, ang):
    x1, x2 = jnp.split(xa, 2, axis=-1)
    cs = jnp.cos(ang)[:, None, :]
    sn = jnp.sin(ang)[:, None, :]
    return jnp.concatenate([x1 * cs - x2 * sn, x1 * sn + x2 * cs], axis=-1)


def apply_axial_rope(x, ang_row, ang_col):
    xf = x.astype(jnp.float32)
    out = jnp.concatenate([rotate_half_axis(xf[..., :AXIS_ROPE_DIM], ang_row),
                           rotate_half_axis(xf[..., AXIS_ROPE_DIM:], ang_col)], axis=-1)
    return out.astype(x.dtype)


def gqa_softmax(q, k, v):
    s = jnp.einsum('bqhgd,bkhd->bhgqk', q, k).astype(jnp.float32) * (HEAD_DIM ** -0.5)
    p = jax.nn.softmax(s, axis=-1).astype(v.dtype)
    return jnp.einsum('bhgqk,bkhd->bqhgd', p, v)


def attention_mixer(q, k, v, qc, kc, vc, q_g, k_g, ang_row, ang_col, need_ctx):
    bsz, n_tok, _ = q.shape
    ctx_len = qc.shape[1]
    grp = ATT_HEADS // ATT_KV_HEADS

    def heads(t, n):
        return t.reshape(t.shape[0], t.shape[1], n, HEAD_DIM)

    q = apply_axial_rope(rms_norm(heads(q, ATT_HEADS), q_g), ang_row, ang_col)
    k = apply_axial_rope(rms_norm(heads(k, ATT_KV_HEADS), k_g), ang_row, ang_col)
    kc = rms_norm(heads(kc, ATT_KV_HEADS), k_g)
    vc = heads(vc, ATT_KV_HEADS)
    k_all = jnp.concatenate([kc, k], axis=1)
    v_all = jnp.concatenate([vc, heads(v, ATT_KV_HEADS)], axis=1)
    n_blk = n_tok // Q_BLOCK
    qb = q.reshape(bsz, n_blk, Q_BLOCK, ATT_KV_HEADS, grp, HEAD_DIM).swapaxes(0, 1)
    ob = lax.map(lambda qi: gqa_softmax(qi, k_all, v_all), qb)
    y = ob.swapaxes(0, 1).reshape(bsz, n_tok, ATT_W)
    yc = None
    if need_ctx:
        qch = rms_norm(heads(qc, ATT_HEADS), q_g).reshape(bsz, ctx_len, ATT_KV_HEADS, grp, HEAD_DIM)
        yc = gqa_softmax(qch, kc, vc).reshape(bsz, ctx_len, ATT_W)
    return y, yc


def hyena_filters(n_tok, w1, b1, fr1, w2, b2, fr2, w3):
    f32 = jnp.float32
    pos = jnp.arange(n_tok, dtype=f32)
    t = pos / max(n_tok - 1, 1)
    bands = jnp.linspace(1e-4, HY_BANDS - 1, HY_BANDS, dtype=f32)
    ang = (2.0 * math.pi / n_tok) * pos[:, None] * bands
    z = jnp.concatenate([t[:, None], jnp.cos(ang), jnp.sin(ang)], axis=-1)
    h = jnp.sin(fr1.astype(f32) * (z @ w1.astype(f32) + b1.astype(f32)))
    h = jnp.sin(fr2.astype(f32) * (h @ w2.astype(f32) + b2.astype(f32)))
    h = h @ w3.astype(f32)
    deltas = jnp.abs(jnp.linspace(math.log(HY_DECAY_TARGET) / HY_DECAY_SLOW,
                                  math.log(HY_DECAY_TARGET) / HY_DECAY_FAST, HY_W, dtype=f32))
    h = h.reshape(n_tok, 2, HY_ORDER, HY_W) * jnp.exp(-t[:, None, None, None] * deltas)
    k_full = jnp.concatenate([h[:, 0], jnp.zeros((1, HY_ORDER, HY_W), f32), h[:0:-1, 1]], axis=0)
    return k_full / (jnp.sum(jnp.abs(k_full), axis=0, keepdims=True) + 1e-6)


def fft_long_conv(u, k_full, d):
    n_tok = u.shape[1]
    uf = u.astype(jnp.float32)
    spec = jnp.fft.rfft(uf, n=2 * n_tok, axis=1) * jnp.fft.rfft(k_full, axis=0)
    y = jnp.fft.irfft(spec, n=2 * n_tok, axis=1)[:, :n_tok]
    return (y + uf * d.astype(jnp.float32)).astype(u.dtype)


def hyena_sequence(v, x1, x2, conv_w, conv_b, k_full, hy_d):
    u = centred_depthwise_conv(jnp.concatenate([v, x1, x2], axis=-1), conv_w) + conv_b
    v, x1, x2 = jnp.split(u, 3, axis=-1)
    z = x1 * fft_long_conv(v, k_full[:, 0], hy_d[0])
    return x2 * fft_long_conv(z, k_full[:, 1], hy_d[1])


def gated_delta_chunked(q, k, v, g, beta, state):
    f32 = jnp.float32
    bsz, n_tok, n_h, dk = q.shape
    dv = v.shape[-1]
    cl = DN_CHUNK
    n_ch = n_tok // cl

    def chunks(a):
        return jnp.moveaxis(a.reshape((bsz, n_ch, cl, n_h) + a.shape[3:]), 3, 1)

    q = chunks(q) * (dk ** -0.5)
    k = chunks(k)
    v = chunks(v)
    g = chunks(g)
    beta = chunks(beta)
    gc = jnp.cumsum(g, axis=-1)
    incl = jnp.tril(jnp.ones((cl, cl), bool))
    strict = jnp.tril(jnp.ones((cl, cl), bool), -1)
    decay = jnp.exp(jnp.where(incl, gc[..., :, None] - gc[..., None, :], -jnp.inf))
    kb = k * beta[..., None]
    a_mat = jnp.where(strict, jnp.einsum('bhncd,bhnsd->bhncs', kb, k) * decay, 0.0)
    eye = jnp.eye(cl, dtype=f32)
    t_mat = lax.linalg.triangular_solve(a_mat + eye, jnp.broadcast_to(eye, a_mat.shape),
                                        left_side=True, lower=True, unit_diagonal=True)
    u = t_mat @ (v * beta[..., None])
    w = t_mat @ (kb * jnp.exp(gc)[..., None])
    qk = jnp.einsum('bhncd,bhnsd->bhncs', q, k) * decay
    q_dec = q * jnp.exp(gc)[..., None]
    k_dec = k * jnp.exp(gc[..., -1:] - gc)[..., None]
    g_tot = jnp.exp(gc[..., -1])
    xs = tuple(jnp.moveaxis(a, 2, 0) for a in (u, w, qk, q_dec, k_dec, g_tot))

    def step(s, inp):
        u_n, w_n, qk_n, qd_n, kd_n, gt_n = inp
        v_new = u_n - jnp.einsum('bhck,bhkv->bhcv', w_n, s)
        o_n = jnp.einsum('bhck,bhkv->bhcv', qd_n, s) + jnp.einsum('bhcs,bhsv->bhcv', qk_n, v_new)
        s = s * gt_n[..., None, None] + jnp.einsum('bhck,bhcv->bhkv', kd_n, v_new)
        return s, o_n

    state, o = lax.scan(step, state, xs)
    o = jnp.moveaxis(jnp.moveaxis(o, 0, 2), 1, 3).reshape(bsz, n_tok, n_h, dv)
    return o, state


def deltanet_prepare(q, k, v, a, b, conv_w, a_log, dt_bias):
    f32 = jnp.float32
    bsz, n_tok, _ = q.shape
    u = jax.nn.silu(centred_depthwise_conv(jnp.concatenate([q, k, v], axis=-1), conv_w))
    q, k, v = jnp.split(u, [DN_QK_W, 2 * DN_QK_W], axis=-1)
    rep = DN_V_HEADS // DN_QK_HEADS
    q = jnp.repeat(l2_normalize(q.reshape(bsz, n_tok, DN_QK_HEADS, DN_HEAD_DIM)), rep, axis=2)
    k = jnp.repeat(l2_normalize(k.reshape(bsz, n_tok, DN_QK_HEADS, DN_HEAD_DIM)), rep, axis=2)
    v = v.reshape(bsz, n_tok, DN_V_HEADS, DN_HEAD_DIM).astype(f32)
    a = a.astype(f32).reshape(bsz, n_tok, 2, DN_V_HEADS)
    g = -jnp.exp(a_log.astype(f32)) * jax.nn.softplus(a + dt_bias.astype(f32))
    beta = jax.nn.sigmoid(b.astype(f32).reshape(bsz, n_tok, 2, DN_V_HEADS))
    return q, k, v, g, beta


def deltanet_gated_out(o, z, norm_g):
    bsz, n_tok = z.shape[:2]
    zh = z.reshape(bsz, n_tok, DN_V_HEADS, DN_HEAD_DIM).astype(jnp.float32)
    y = rms_norm(o, norm_g) * jax.nn.silu(zh)
    return y.astype(z.dtype).reshape(bsz, n_tok, DN_V_W)


def deltanet_mixer(q, k, v, z, a, b, qc, kc, vc, zc, ac, bc, conv_w, a_log, dt_bias, norm_g, need_ctx):
    q, k, v, g, beta = deltanet_prepare(q, k, v, a, b, conv_w, a_log, dt_bias)
    qc, kc, vc, gcx, bcx = deltanet_prepare(qc, kc, vc, ac, bc, conv_w, a_log, dt_bias)
    s0 = jnp.zeros((q.shape[0], DN_V_HEADS, DN_HEAD_DIM, DN_HEAD_DIM), jnp.float32)

    def flip(t):
        return jnp.flip(t, axis=1)

    oc_f, s_f = gated_delta_chunked(qc, kc, vc, gcx[:, :, 0], bcx[:, :, 0], s0)
    oc_b, s_b = gated_delta_chunked(flip(qc), flip(kc), flip(vc), flip(gcx[:, :, 1]), flip(bcx[:, :, 1]), s0)
    o_f, _ = gated_delta_chunked(q, k, v, g[:, :, 0], beta[:, :, 0], s_f)
    o_b, _ = gated_delta_chunked(flip(q), flip(k), flip(v), flip(g[:, :, 1]), flip(beta[:, :, 1]), s_b)
    y = deltanet_gated_out(o_f + flip(o_b), z, norm_g)
    yc = deltanet_gated_out(oc_f + flip(oc_b), zc, norm_g) if need_ctx else None
    return y, yc


def merge_branches(ya, yb, yc, merge_logits, w_pa, w_pb, w_pc, w_out):
    gates = jax.nn.sigmoid(merge_logits.astype(jnp.float32)).astype(merge_logits.dtype)
    g_a, g_b, g_c = jnp.split(gates, N_BRANCH, axis=-1)
    m = g_a * (ya @ w_pa) + g_b * (yb @ w_pb) + g_c * (yc @ w_pc)
    return m @ w_out


def setup_inputs(seed: int = 0) -> dict:
    key = jax.random.key(seed)
    ks = jax.random.split(key, 32)
    f32 = jnp.float32
    D = D_MODEL

    def nrm(k, shape, scale):
        return jax.random.normal(k, shape, f32) * scale

    dt = jnp.exp(jax.random.uniform(ks[20], (DEPTH, 2, DN_V_HEADS), f32, math.log(1e-3), math.log(1e-1)))
    return {
        'x': nrm(ks[0], (BATCH, SEQ, D), 1.0),
        'c': nrm(ks[1], (BATCH, D), 1.0),
        'ctx': nrm(ks[2], (BATCH, CTX_LEN, D), 1.0),
        'c_ctx': nrm(ks[3], (D,), 1.0),
        'norm_g': 1.0 + nrm(ks[4], (DEPTH, D), 0.02),
        'w_mod': nrm(ks[5], (DEPTH, D, 3 * D), 0.5 * D ** -0.5),
        'b_mod': nrm(ks[6], (DEPTH, 3 * D), 0.02),
        'w_in': nrm(ks[7], (DEPTH, D, IN_W), D ** -0.5),
        'q_norm_g': 1.0 + nrm(ks[8], (DEPTH, HEAD_DIM), 0.02),
        'k_norm_g': 1.0 + nrm(ks[9], (DEPTH, HEAD_DIM), 0.02),
        'hy_conv_w': nrm(ks[10], (DEPTH, HY_SHORT, 3 * HY_W), HY_SHORT ** -0.5),
        'hy_conv_b': nrm(ks[11], (DEPTH, 3 * HY_W), 0.02),
        'hy_w1': nrm(ks[12], (DEPTH, HY_EMB, HY_FILTER_HIDDEN), HY_EMB ** -0.5),
        'hy_b1': nrm(ks[13], (DEPTH, HY_FILTER_HIDDEN), 0.1),
        'hy_freq1': 1.0 + nrm(ks[14], (DEPTH, HY_FILTER_HIDDEN), 0.02),
        'hy_w2': nrm(ks[15], (DEPTH, HY_FILTER_HIDDEN, HY_FILTER_HIDDEN), HY_FILTER_HIDDEN ** -0.5),
        'hy_b2': nrm(ks[16], (DEPTH, HY_FILTER_HIDDEN), 0.1),
        'hy_freq2': 1.0 + nrm(ks[17], (DEPTH, HY_FILTER_HIDDEN), 0.02),
        'hy_w3': nrm(ks[18], (DEPTH, HY_FILTER_HIDDEN, 2 * HY_ORDER * HY_W), HY_FILTER_HIDDEN ** -0.5),
        'hy_d': nrm(ks[19], (DEPTH, HY_ORDER, HY_W), 0.5),
        'dn_conv_w': nrm(ks[21], (DEPTH, DN_SHORT, 2 * DN_QK_W + DN_V_W), DN_SHORT ** -0.5),
        'dn_a_log': jnp.log(jax.random.uniform(ks[22], (DEPTH, 2, DN_V_HEADS), f32, 1.0, 16.0)),
        'dn_dt_bias': dt + jnp.log(-jnp.expm1(-dt)),
        'dn_norm_g': 1.0 + nrm(ks[23], (DEPTH, DN_HEAD_DIM), 0.02),
        'w_pa': nrm(ks[24], (DEPTH, ATT_W, D), ATT_W ** -0.5),
        'w_pb': nrm(ks[25], (DEPTH, HY_W, D), HY_W ** -0.5),
        'w_pc': nrm(ks[26], (DEPTH, DN_V_W, D), DN_V_W ** -0.5),
        'w_out': nrm(ks[27], (DEPTH, D, D), D ** -0.5),
        'final_g': 1.0 + nrm(ks[28], (D,), 0.02),
    }


def reference(x, c, ctx, c_ctx, norm_g, w_mod, b_mod, w_in, q_norm_g, k_norm_g,
              hy_conv_w, hy_conv_b, hy_w1, hy_b1, hy_freq1, hy_w2, hy_b2, hy_freq2, hy_w3, hy_d,
              dn_conv_w, dn_a_log, dn_dt_bias, dn_norm_g, w_pa, w_pb, w_pc, w_out, final_g):
    f32 = jnp.float32
    n_tok = x.shape[1]
    ctx_len = ctx.shape[1]
    rows = n_tok // GRID_W
    row_idx = jnp.repeat(jnp.arange(rows, dtype=f32), GRID_W)
    col_idx = jnp.tile(jnp.arange(GRID_W, dtype=f32), rows)
    inv_freq = ROPE_THETA ** (-jnp.arange(0, AXIS_ROPE_DIM, 2, dtype=f32) / AXIS_ROPE_DIM)
    ang_row = row_idx[:, None] * inv_freq
    ang_col = col_idx[:, None] * inv_freq
    split_at = [int(i) for i in np.cumsum(IN_WIDTHS)[:-1]]

    xc = ctx
    for layer in range(DEPTH):
        need_ctx = layer < DEPTH - 1
        shift, scale, gate = ada_modulation(c, w_mod[layer], b_mod[layer])
        shift_c, scale_c, gate_c = ada_modulation(c_ctx, w_mod[layer], b_mod[layer])
        h = rms_norm(x, norm_g[layer]) * (1.0 + scale[:, None]) + shift[:, None]
        hc = rms_norm(xc, norm_g[layer]) * (1.0 + scale_c) + shift_c
        p = jnp.split(h @ w_in[layer], split_at, axis=-1)
        pc = jnp.split(hc @ w_in[layer], split_at, axis=-1)

        att, att_c = attention_mixer(p[0], p[1], p[2], pc[0], pc[1], pc[2],
                                     q_norm_g[layer], k_norm_g[layer], ang_row, ang_col, need_ctx)
        ya = att * jax.nn.silu(p[3])

        filt_args = (hy_w1[layer], hy_b1[layer], hy_freq1[layer], hy_w2[layer], hy_b2[layer],
                     hy_freq2[layer], hy_w3[layer])
        hy = hyena_sequence(p[4], p[5], p[6], hy_conv_w[layer], hy_conv_b[layer],
                            hyena_filters(n_tok, *filt_args), hy_d[layer])
        yb = hy * jax.nn.silu(p[7])

        yc, dn_c = deltanet_mixer(p[8], p[9], p[10], p[11], p[12], p[13],
                                  pc[8], pc[9], pc[10], pc[11], pc[12], pc[13],
                                  dn_conv_w[layer], dn_a_log[layer], dn_dt_bias[layer],
                                  dn_norm_g[layer], need_ctx)

        out = merge_branches(ya, yb, yc, p[14], w_pa[layer], w_pb[layer], w_pc[layer], w_out[layer])
        if need_ctx:
            hy_c = hyena_sequence(pc[4], pc[5], pc[6], hy_conv_w[layer], hy_conv_b[layer],
                                  hyena_filters(ctx_len, *filt_args), hy_d[layer])
            out_c = merge_branches(att_c * jax.nn.silu(pc[3]), hy_c * jax.nn.silu(pc[7]), dn_c, pc[14],
                                   w_pa[layer], w_pb[layer], w_pc[layer], w_out[layer])
            xc = xc + gate_c * out_c
        x = x + gate[:, None] * out

    return rms_norm(x, final_g)
```

```python
from contextlib import ExitStack
import math
import numpy as np
import ml_dtypes
import concourse.bass as bass
import concourse.mybir as mybir
from concourse.alu_op_type import AluOpType as ALU
from concourse.bass_utils import run_bass_kernel_spmd

F32 = mybir.dt.float32
BF16 = mybir.dt.bfloat16
AF = mybir.ActivationFunctionType
AX = mybir.AxisListType
ENGS = ("pe", "dve", "act", "pool", "sp")
NDS = 6
ARENA_WORDS = 44 * 1024


class Res:
    __slots__ = ("w", "rd")

    def __init__(self):
        self.w = None
        self.rd = []


class Tile:
    def __init__(self, ap, nres=1):
        self.ap = ap
        self.res = [Res() for _ in range(nres)]
        self.r = self.res[0]

    def __getitem__(self, k):
        return self.ap[k]


class Prog:
    def __init__(self):
        self.nc = bass.Bass("TRN2", target_bir_lowering=False)
        self.st = ExitStack()
        nc = self.nc
        self.eng = {"pe": nc.tensor, "dve": nc.vector, "act": nc.scalar, "pool": nc.gpsimd, "sp": nc.sync}
        self.sem = {e: self.st.enter_context(nc.semaphore("s_" + e)) for e in ENGS}
        self.cnt = {e: 0 for e in ENGS}
        self.seen = {e: {} for e in ENGS}
        self.dsem, self.dcnt, self.dnext = {}, {}, {}
        for qn in ("sp", "pool", "act"):
            self.dsem[qn] = [self.st.enter_context(nc.semaphore(f"d_{qn}{i}")) for i in range(NDS)]
            self.dcnt[qn] = [0] * NDS
            self.dnext[qn] = 0
        self.stacks = [ExitStack()]
        self.used = [0]
        self.nname = 0
        self.pst = None
        self.banks = [Tile(None) for _ in range(8)]
        self._fresh_psum()
        self.ninst = 0
        self.nwait = 0

    def _fresh_psum(self):
        if self.pst is not None:
            self.pst.close()
        self.pst = ExitStack()
        self.nname += 1
        for i in range(8):
            h = self.pst.enter_context(self.nc.psum_tensor(f"ps{self.nname}_{i}", [128, 512], F32))
            self.banks[i].ap = h[:, :]

    def dram(self, name, shape, dt, kind):
        return self.nc.dram_tensor(name, list(shape), dt, kind=kind).ap()

    def mark(self):
        self.stacks.append(ExitStack())
        self.used.append(self.used[-1])
        return len(self.stacks) - 1

    def release(self, m):
        while len(self.stacks) > m:
            self.stacks.pop().close()
            self.used.pop()
        self.stacks.append(ExitStack())
        self.used.append(self.used[-1])
        self.barrier()

    def alloc(self, free_shape, dt, nres=1):
        n = int(np.prod(free_shape))
        nbytes = (n * (2 if dt == BF16 else 4) + 31) // 32 * 32
        self.used[-1] += nbytes
        assert self.used[-1] <= ARENA_WORDS * 4, f"SBUF overflow {self.used[-1]}"
        self.nname += 1
        h = self.stacks[-1].enter_context(self.nc.sbuf_tensor(f"t{self.nname}", [128, n], dt))
        ap = h[:, :]
        if len(free_shape) == 2:
            ap = ap.rearrange("p (a b) -> p a b", a=free_shape[0])
        elif len(free_shape) == 3:
            ap = ap.rearrange("p (a b c) -> p a b c", a=free_shape[0], b=free_shape[1])
        elif len(free_shape) == 4:
            ap = ap.rearrange("p (a b c d) -> p a b c d", a=free_shape[0], b=free_shape[1], c=free_shape[2])
        return Tile(ap, nres)

    def _wait(self, e, tok):
        if tok is None:
            return
        sem, val = tok
        k = id(sem)
        if self.seen[e].get(k, 0) >= val:
            return
        self.seen[e][k] = val
        self.eng[e].wait_ge(sem, val)
        self.nwait += 1

    @staticmethod
    def _rl(xs):
        return [x.r if isinstance(x, Tile) else x for x in xs]

    def _deps(self, e, r, w):
        for x in r:
            self._wait(e, x.w)
        for x in w:
            self._wait(e, x.w)
            for t in x.rd:
                self._wait(e, t)

    @staticmethod
    def _commit(tok, r, w):
        for x in r:
            x.rd.append(tok)
            if len(x.rd) > 32:
                x.rd = x.rd[-32:]
        for x in w:
            x.w = tok
            x.rd = []

    def op(self, e, fn, r=(), w=()):
        r, w = self._rl(r), self._rl(w)
        self._deps(e, r, w)
        self.cnt[e] += 1
        tok = (self.sem[e], self.cnt[e])
        fn(self.eng[e]).then_inc(self.sem[e], 1)
        self._commit(tok, r, w)
        self.ninst += 1
        return tok

    def dma(self, qn, out, in_, r=(), w=(), **kw):
        r, w = self._rl(r), self._rl(w)
        self._deps(qn, r, w)
        i = self.dnext[qn]
        self.dnext[qn] = (i + 1) % NDS
        sem = self.dsem[qn][i]
        if self.dcnt[qn][i] > 0:
            self._wait(qn, (sem, 16 * self.dcnt[qn][i]))
        self.dcnt[qn][i] += 1
        tok = (sem, 16 * self.dcnt[qn][i])
        self.eng[qn].dma_start(out=out, in_=in_, **kw).then_inc(sem, 16)
        self._commit(tok, r, w)
        self.ninst += 1
        return tok

    def barrier(self):
        toks = [(self.sem[e], self.cnt[e]) for e in ENGS if self.cnt[e] > 0]
        for qn in self.dsem:
            for i, sem in enumerate(self.dsem[qn]):
                if self.dcnt[qn][i] > 0:
                    toks.append((sem, 16 * self.dcnt[qn][i]))
        for e in ENGS:
            for t in toks:
                if t[0] is not self.sem[e]:
                    self._wait(e, t)
        self._fresh_psum()

    def finish(self):
        self.barrier()
        while self.stacks:
            self.stacks.pop().close()
        self.pst.close()
        self.st.close()
        return self.nc


D = 2048
NKC = 16
DEPTH = 4
IN_W = 15904
FM0, FM1 = 1536, 8704
NFM = FM1 - FM0
NTM = FM0 + (IN_W - FM1)
TM_Z, TM_A, TM_B, TM_MG = 1536, 2560, 2576, 2592
FM_AG, FM_HV, FM_HX1, FM_HX2, FM_HG, FM_DQ, FM_DK, FM_DV = 0, 1024, 2048, 3072, 4096, 5120, 5632, 6144
EPS = 1e-6


class K:
    def __init__(self, P):
        self.P = P

    def mm(self, out, lhsT, rhs, start, stop, r, w):
        self.P.op("pe", lambda e: e.matmul(out=out, lhsT=lhsT, rhs=rhs, start=start, stop=stop), r, w)

    def tr(self, out, in_, ident, r, w):
        self.P.op("pe", lambda e: e.transpose(out=out, in_=in_, identity=ident), r, w)

    def tt(self, eng, out, in0, in1, op, r, w):
        self.P.op(eng, lambda e: e.tensor_tensor(out=out, in0=in0, in1=in1, op=op), r, w)

    def ts(self, eng, out, in0, s1, s2, op0, op1, r, w):
        if op1 is None:
            self.P.op(eng, lambda e: e.tensor_scalar(out=out, in0=in0, scalar1=s1, scalar2=None, op0=op0), r, w)
        else:
            self.P.op(eng, lambda e: e.tensor_scalar(out=out, in0=in0, scalar1=s1, scalar2=s2, op0=op0, op1=op1), r, w)

    def stt(self, out, in0, scalar, in1, op0, op1, r, w):
        self.P.op("dve", lambda e: e.scalar_tensor_tensor(out=out, in0=in0, scalar=scalar, in1=in1, op0=op0, op1=op1), r, w)

    def act(self, out, in_, func, r, w, bias=None, scale=None):
        kw = {}
        if bias is not None:
            kw["bias"] = bias
        if scale is not None:
            kw["scale"] = scale
        self.P.op("act", lambda e: e.activation(out=out, in_=in_, func=func, **kw), r, w)

    def copy(self, eng, out, in_, r, w):
        if eng == "act":
            self.P.op("act", lambda e: e.activation(out=out, in_=in_, func=AF.Copy), r, w)
        else:
            self.P.op(eng, lambda e: e.tensor_copy(out=out, in_=in_), r, w)

    def recip(self, out, in_, r, w):
        self.P.op("dve", lambda e: e.reciprocal(out=out, in_=in_), r, w)

    def ssq(self, junk, in_, acc, r, w):
        self.P.op("act", lambda e: e.activation(out=junk, in_=in_, func=AF.Square, accum_out=acc), r, w)

    def memset(self, eng, ap, val, w):
        self.P.op(eng, lambda e: e.memset(ap, val), (), w)


def bank_bf(P, i):
    return P.banks[i].ap.bitcast(BF16)


def tok_groups(n, g=512):
    out, t0 = [], 0
    while t0 < n:
        out.append((t0, min(g, n - t0)))
        t0 += g
    return out


def build(cfg):
    T_CTX, T_LAT, depth = cfg["t_ctx"], cfg["t_lat"], cfg["depth"]
    dbg = cfg.get("dbg", ())
    T = T_CTX + T_LAT
    NT = T // 128
    NCT = T_CTX // 128
    HALF = cfg.get("half", 17)
    assert NT % HALF == 0
    P = Prog()
    k = K(P)
    def inp(n, s, dt=F32):
        if n in cfg.get("dummy", ()):
            s = [1] * len(s)
        return P.dram(n, s, dt, "ExternalInput")
    WD = cfg.get("wdepth", DEPTH)
    x_in = inp("x", [T_LAT, D])
    ctx_in = inp("ctx", [T_CTX, D])
    cs_in = inp("csT", [128, NKC, 2])
    norm_g = inp("norm_g", [WD, D])
    w_mod = inp("w_mod", [WD, D, 3 * D])
    b_mod = inp("b_mod", [WD, 3 * D])
    w_in = inp("w_in", [WD, D, IN_W])
    w_pa = inp("w_pa", [WD, 1024, D])
    w_pb = inp("w_pb", [WD, 1024, D])
    w_pc = inp("w_pc", [WD, 1024, D])
    w_out = inp("w_out", [WD, D, D])
    final_g = inp("final_g", [D])
    ident_in = inp("ident", [128, 128])
    out = P.dram("out", [T_LAT, D], F32, "ExternalOutput")

    def scratch(n, s, dt=F32):
        kind = "ExternalOutput" if n in dbg else ("ExternalInput" if n in cfg.get("ext_in", ()) else "Internal")
        return P.dram(n, s, dt, kind)

    xs = scratch("xs", [T, D])
    pTM = scratch("pTM", [T, NTM])
    pFM = scratch("pFM", [NFM, T])
    modS = scratch("modS", [2, 3 * D])
    yT = scratch("yT", [3072, T], BF16)
    mT = scratch("mT", [D, T], BF16)
    R_xs, R_pTM, R_pFM, R_modS, R_yT, R_mT = Res(), Res(), Res(), Res(), Res(), Res()

    identf = P.alloc([128], F32)
    identb = P.alloc([128], BF16)
    cs = P.alloc([NKC, 2], F32)
    P.dma("sp", identf[:], ident_in[:, :], w=[identf])
    P.dma("pool", identb[:], ident_in[:, :], w=[identb])
    P.dma("sp", cs[:], cs_in[:, :, :], w=[cs])
    k.act(cs[:], cs[:], AF.Silu, [cs], [cs])
    base_mark = P.mark()
    rr = {"bank": 0, "ev": 0}

    def next_bank(lo=2, n=4):
        b = P.banks[lo + rr["bank"] % n]
        rr["bank"] += 1
        return b

    def evac(out_ap, in_ap, r, w):
        eng = "act" if rr["ev"] % 2 == 0 else "dve"
        rr["ev"] += 1
        k.copy(eng, out_ap, in_ap, r, w)

    def xrows(l, t):
        if l == 0:
            if t < NCT:
                return ctx_in[t * 128:(t + 1) * 128, :]
            return x_in[(t - NCT) * 128:(t - NCT + 1) * 128, :]
        return xs[t * 128:(t + 1) * 128, :]

    def phase_mod(l):
        m0 = P.mark()
        bm = P.alloc([3 * D], F32)
        mo = P.alloc([3 * D], F32)
        wst = [P.alloc([NKC, 512], F32) for _ in range(2)]
        P.dma("sp", bm[0:2, :], b_mod[l].partition_broadcast(2), w=[bm])
        for cb in range(12):
            ws = wst[cb % 2]
            P.dma("sp", ws[:], w_mod[l][:, cb * 512:(cb + 1) * 512].rearrange("(kc p) n -> p kc n", p=128), w=[ws])
            bank = P.banks[6 + cb % 2]
            for kc in range(NKC):
                k.mm(bank[0:2, :], cs[:, kc, :], ws[:, kc, :], kc == 0, kc == NKC - 1, [cs, ws], [bank])
            k.tt("dve", mo[0:2, cb * 512:(cb + 1) * 512], bank[0:2, :], bm[0:2, cb * 512:(cb + 1) * 512], ALU.add,
                 [bank, bm], [mo])
        P.dma("sp", modS[:, :], mo[0:2, :], r=[mo], w=[R_modS])
        P.release(m0)

    def phase_inproj(l):
        m0 = P.mark()
        gbc = P.alloc([D], F32)
        A = [P.alloc([D], F32) for _ in range(2)]
        B = [P.alloc([D], F32) for _ in range(2)]
        P.dma("sp", gbc[:], norm_g[l].partition_broadcast(128), w=[gbc])
        for wch in range(2):
            P.dma("sp", A[wch][:], modS[wch, D:2 * D].partition_broadcast(128), r=[R_modS], w=[A[wch]])
            P.dma("sp", B[wch][:], modS[wch, 0:D].partition_broadcast(128), r=[R_modS], w=[B[wch]])
            k.stt(A[wch][:], A[wch][:], 1.0, gbc[:], ALU.add, ALU.mult, [A[wch], gbc], [A[wch]])
        HT = HALF * 128
        hT = P.alloc([NKC, HT], BF16)
        xt1 = P.alloc([D], F32)
        xt2 = [xt1, xt1]
        tmp = P.alloc([D], F32)
        hb = P.alloc([D], BF16)
        ss = P.alloc([2], F32)
        wb2 = [P.alloc([NKC, 512], BF16) for _ in range(2)]
        ot4 = [P.alloc([512], F32) for _ in range(4)]
        blocks = []
        for c0 in range(0, IN_W, 512):
            n = min(512, IN_W - c0)
            if FM0 <= c0 < FM1:
                blocks.append(("fm", c0, n, c0 - FM0))
            elif c0 < FM0:
                blocks.append(("tm", c0, n, c0))
            else:
                blocks.append(("tm", c0, n, FM0 + c0 - FM1))
        oi = 0
        for half in range(NT // HALF):
            for tl in range(HALF):
                t = half * HALF + tl
                xt = xt2[tl % 2]
                P.dma("sp", xt[:], xrows(l, t), r=[R_xs], w=[xt])
                k.ssq(tmp[:], xt[:], ss[:, 0:1], [xt], [tmp, ss])
                k.ts("dve", ss[:, 0:1], ss[:, 0:1], 1.0 / D, EPS, ALU.mult, ALU.add, [ss], [ss])
                k.act(ss[:, 0:1], ss[:, 0:1], AF.Sqrt, [ss], [ss])
                k.recip(ss[:, 1:2], ss[:, 0:1], [ss], [ss])
                wch = 1 if t < NCT else 0
                k.stt(tmp[:], xt[:], ss[:, 1:2], A[wch][:], ALU.mult, ALU.mult, [xt, ss, A[wch]], [tmp])
                k.tt("dve", hb[:], tmp[:], B[wch][:], ALU.add, [tmp, B[wch]], [hb])
                for g in range(2):
                    bank = P.banks[g]
                    bv = bank_bf(P, g).rearrange("p (a b) -> p a b", a=8)
                    for j in range(8):
                        kc = g * 8 + j
                        k.tr(bv[:, j, :], hb[:, kc * 128:(kc + 1) * 128], identb[:], [hb, identb], [bank])
                    evac(hT[:, g * 8:(g + 1) * 8, tl * 128:(tl + 1) * 128], bv[:, :, :], [bank], [hT])
            for bi, (kind, c0, n, dst) in enumerate(blocks):
                wb = wb2[bi % 2]
                P.dma("pool", wb[:, :, 0:n], w_in[l][:, c0:c0 + n].rearrange("(kc p) n -> p kc n", p=128), w=[wb])
                if kind == "tm":
                    for tl in range(HALF):
                        t = half * HALF + tl
                        bank = next_bank()
                        for kc in range(NKC):
                            k.mm(bank[:, 0:n], hT[:, kc, tl * 128:(tl + 1) * 128], wb[:, kc, 0:n], kc == 0, kc == NKC - 1,
                                 [hT, wb], [bank])
                        ot = ot4[oi % 4]
                        oi += 1
                        evac(ot[:, 0:n], bank[:, 0:n], [bank], [ot])
                        P.dma("sp", pTM[t * 128:(t + 1) * 128, dst:dst + n], ot[:, 0:n], r=[ot], w=[R_pTM])
                else:
                    for ch in range(n // 128):
                        for (t0, tn) in tok_groups(HT):
                            bank = next_bank()
                            for kc in range(NKC):
                                k.mm(bank[:, 0:tn], wb[:, kc, ch * 128:(ch + 1) * 128], hT[:, kc, t0:t0 + tn], kc == 0,
                                     kc == NKC - 1, [hT, wb], [bank])
                            ot = ot4[oi % 4]
                            oi += 1
                            evac(ot[:, 0:tn], bank[:, 0:tn], [bank], [ot])
                            P.dma("sp", pFM[dst + ch * 128:dst + (ch + 1) * 128, half * HT + t0:half * HT + t0 + tn],
                                  ot[:, 0:tn], r=[ot], w=[R_pFM])
        P.release(m0)

    def phase_merge(l, last):
        m0 = P.mark()
        t_first = NCT if last else 0
        wp = [P.alloc([8, D], BF16) for _ in range(3)]
        for br, wsrc in enumerate((w_pa, w_pb, w_pc)):
            for hh in range(2):
                P.dma("pool", wp[br][:, hh * 4:(hh + 1) * 4, :],
                      wsrc[l][hh * 512:(hh + 1) * 512, :].rearrange("(kc p) n -> p kc n", p=128), w=[wp[br]])
        yt2 = [P.alloc([24, 128], BF16) for _ in range(2)]
        lg1 = P.alloc([3 * D], F32)
        lg2 = [lg1, lg1]
        macc = P.alloc([D], F32)
        mtmp = P.alloc([512], F32)
        mb = P.alloc([D], BF16)
        mTt = [P.alloc([NKC, 128], BF16) for _ in range(2)]
        for t in range(t_first, NT):
            yt = yt2[t % 2]
            lg = lg2[t % 2]
            P.dma("sp", yt[:], yT[:, t * 128:(t + 1) * 128].rearrange("(c p) t -> p c t", p=128), r=[R_yT], w=[yt])
            P.dma("sp", lg[:], pTM[t * 128:(t + 1) * 128, TM_MG:TM_MG + 3 * D], r=[R_pTM], w=[lg])
            k.act(lg[:], lg[:], AF.Sigmoid, [lg], [lg])
            for cb in range(4):
                cs_ = slice(cb * 512, (cb + 1) * 512)
                for br in range(3):
                    bank = next_bank()
                    for kc in range(8):
                        k.mm(bank[:, :], yt[:, br * 8 + kc, :], wp[br][:, kc, cs_], kc == 0, kc == 7, [yt, wp[br]], [bank])
                    gs = slice(br * D + cb * 512, br * D + (cb + 1) * 512)
                    if br == 0:
                        k.tt("dve", macc[:, cs_], bank[:, :], lg[:, gs], ALU.mult, [bank, lg], [macc])
                    else:
                        k.tt("dve", mtmp[:], bank[:, :], lg[:, gs], ALU.mult, [bank, lg], [mtmp])
                        k.tt("pool", macc[:, cs_], macc[:, cs_], mtmp[:], ALU.add, [macc, mtmp], [macc])
            k.copy("act", mb[:], macc[:], [macc], [mb])
            mt = mTt[t % 2]
            for g in range(2):
                bank = P.banks[g]
                bv = bank_bf(P, g).rearrange("p (a b) -> p a b", a=8)
                for j in range(8):
                    kc = g * 8 + j
                    k.tr(bv[:, j, :], mb[:, kc * 128:(kc + 1) * 128], identb[:], [mb, identb], [bank])
                evac(mt[:, g * 8:(g + 1) * 8, :], bv[:, :, :], [bank], [mt])
            P.dma("sp", mT[:, t * 128:(t + 1) * 128].rearrange("(c p) t -> p c t", p=128), mt[:], r=[mt], w=[R_mT])
        P.release(m0)
        P.barrier()
        wo = P.alloc([NKC, D], BF16)
        for hh in range(4):
            P.dma("pool", wo[:, hh * 4:(hh + 1) * 4, :],
                  w_out[l][hh * 512:(hh + 1) * 512, :].rearrange("(kc p) n -> p kc n", p=128), w=[wo])
        gate = [P.alloc([D], F32) for _ in range(2)]
        for wch in range(2):
            P.dma("sp", gate[wch][:], modS[wch, 2 * D:3 * D].partition_broadcast(128), r=[R_modS], w=[gate[wch]])
        if last:
            fg = P.alloc([D], F32)
            P.dma("sp", fg[:], final_g.partition_broadcast(128), w=[fg])
        mt2 = [P.alloc([NKC, 128], BF16) for _ in range(2)]
        xt2 = [P.alloc([D], F32) for _ in range(2)]
        xn2 = [P.alloc([D], F32) for _ in range(2)]
        tmp = P.alloc([D], F32)
        ss = P.alloc([2], F32)
        for t in range(t_first, NT):
            mt, xt, xn = mt2[t % 2], xt2[t % 2], xn2[t % 2]
            wch = 1 if t < NCT else 0
            P.dma("sp", mt[:], mT[:, t * 128:(t + 1) * 128].rearrange("(c p) t -> p c t", p=128), r=[R_mT], w=[mt])
            P.dma("sp", xt[:], xrows(l, t), r=[R_xs], w=[xt])
            for cb in range(4):
                cs_ = slice(cb * 512, (cb + 1) * 512)
                bank = next_bank()
                for kc in range(NKC):
                    k.mm(bank[:, :], mt[:, kc, :], wo[:, kc, cs_], kc == 0, kc == NKC - 1, [mt, wo], [bank])
                k.tt("dve", xn[:, cs_], bank[:, :], gate[wch][:, cs_], ALU.mult, [bank, gate[wch]], [xn])
                k.tt("pool", xn[:, cs_], xn[:, cs_], xt[:, cs_], ALU.add, [xn, xt], [xn])
            if not last:
                P.dma("sp", xs[t * 128:(t + 1) * 128, :], xn[:], r=[xn], w=[R_xs])
            else:
                k.ssq(tmp[:], xn[:], ss[:, 0:1], [xn], [tmp, ss])
                k.ts("dve", ss[:, 0:1], ss[:, 0:1], 1.0 / D, EPS, ALU.mult, ALU.add, [ss], [ss])
                k.act(ss[:, 0:1], ss[:, 0:1], AF.Sqrt, [ss], [ss])
                k.recip(ss[:, 1:2], ss[:, 0:1], [ss], [ss])
                k.stt(tmp[:], xn[:], ss[:, 1:2], fg[:], ALU.mult, ALU.mult, [xn, ss, fg], [tmp])
                P.dma("sp", out[(t - NCT) * 128:(t - NCT + 1) * 128, :], tmp[:], r=[tmp])
        P.release(m0)

    env = dict(P=P, k=k, cfg=cfg, T=T, NT=NT, NCT=NCT, T_CTX=T_CTX, T_LAT=T_LAT, pTM=pTM, pFM=pFM, yT=yT,
               R_pTM=R_pTM, R_pFM=R_pFM, R_yT=R_yT, identf=identf, identb=identb, next_bank=next_bank, evac=evac,
               inp=inp)
    mixers = cfg.get("mixers", ("att", "hy", "dn"))
    mix_fns = {}
    if "att" in mixers:
        mix_fns["att"] = make_attention(env)
    if "hy" in mixers:
        mix_fns["hy"] = make_hyena(env)
    if "dn" in mixers:
        mix_fns["dn"] = make_deltanet(env)
    phases = cfg.get("phases", ("mod", "inproj", "mix", "merge"))
    for l in range(depth):
        last = (l == DEPTH - 1) if cfg.get("real_last", True) else (l == depth - 1)
        if "mod" in phases:
            phase_mod(l)
            P.barrier()
        if "inproj" in phases:
            phase_inproj(l)
            P.barrier()
        if "mix" in phases:
            for name in mixers:
                mix_fns[name](l, last)
                P.barrier()
        if "merge" in phases:
            phase_merge(l, last)
            P.barrier()
    print("instructions:", P.ninst, "waits:", P.nwait)
    return P.finish()


def make_attention(env):
    P, k, cfg = env["P"], env["k"], env["cfg"]
    T, NT, NCT, T_CTX, T_LAT = env["T"], env["NT"], env["NCT"], env["T_CTX"], env["T_LAT"]
    pTM, pFM, yT = env["pTM"], env["pFM"], env["yT"]
    R_pTM, R_pFM, R_yT = env["R_pTM"], env["R_pFM"], env["R_yT"]
    identb, evac = env["identb"], env["evac"]
    WD = cfg.get("wdepth", DEPTH)
    q_norm_g = env["inp"]("q_norm_g", [WD, 128])
    k_norm_g = env["inp"]("k_norm_g", [WD, 128])
    rope = env["inp"]("rope", [T_LAT, 2, 128])

    def run(l, last):
        m0 = P.mark()
        qT = P.alloc([8, T], BF16)
        kT = P.alloc([2, T], BF16)
        Vb = P.alloc([NT, 256], BF16)
        gqk = P.alloc([10, 128], F32)
        onesb = P.alloc([128], BF16)
        k.memset("dve", onesb[:], 1.0, [onesb])
        for h in range(10):
            src = q_norm_g[l] if h < 8 else k_norm_g[l]
            P.dma("sp", gqk[:, h, :], src.partition_broadcast(128), w=[gqk])
        k.ts("dve", gqk[:, 0:8, :], gqk[:, 0:8, :], 128.0 ** -0.5, None, ALU.mult, None, [gqk], [gqk])
        m1 = P.mark()
        qk2 = [P.alloc([10, 128], F32) for _ in range(2)]
        cs2 = [P.alloc([2, 128], F32) for _ in range(2)]
        sq = P.alloc([10, 128], F32)
        xn = P.alloc([10, 128], F32)
        t1 = P.alloc([10, 128], F32)
        t2 = P.alloc([10, 128], F32)
        ob = P.alloc([10, 128], BF16)
        st = P.alloc([2, 10], F32)
        for t in range(NT):
            qk = qk2[t % 2]
            rows = slice(t * 128, (t + 1) * 128)
            P.dma("sp", qk[:], pTM[rows, 0:1280].rearrange("p (h d) -> p h d", h=10), r=[R_pTM], w=[qk])
            P.dma("pool", Vb[:, t, :], pTM[rows, 1280:1536], r=[R_pTM], w=[Vb])
            k.tt("pool", sq[:], qk[:], qk[:], ALU.mult, [qk], [sq])
            P.op("dve", lambda e, o=st[:, 0, :], i=sq[:]: e.tensor_reduce(out=o, in_=i, axis=AX.X, op=ALU.add), [sq], [st])
            k.ts("dve", st[:, 0, :], st[:, 0, :], 1.0 / 128, EPS, ALU.mult, ALU.add, [st], [st])
            k.act(st[:, 0, :], st[:, 0, :], AF.Sqrt, [st], [st])
            k.recip(st[:, 1, :], st[:, 0, :], [st], [st])
            k.tt("dve", xn[:], qk[:], st[:, 1, :].unsqueeze(2).to_broadcast([128, 10, 128]), ALU.mult, [qk, st], [xn])
            if t < NCT:
                k.tt("dve", ob[:], xn[:], gqk[:], ALU.mult, [xn, gqk], [ob])
            else:
                cs_ = cs2[t % 2]
                P.dma("sp", cs_[:], rope[(t - NCT) * 128:(t - NCT + 1) * 128, :, :], w=[cs_])
                k.tt("pool", xn[:], xn[:], gqk[:], ALU.mult, [xn, gqk], [xn])
                k.tt("dve", t1[:], xn[:], cs_[:, 0:1, :].to_broadcast([128, 10, 128]), ALU.mult, [xn, cs_], [t1])
                xn4 = xn[:].rearrange("p h (a j) -> p h a j", a=2)
                t24 = t2[:].rearrange("p h (a j) -> p h a j", a=2)
                s4 = cs_[:, 1:2, :].rearrange("p o (a j) -> p o a j", a=2)
                k.tt("pool", t24[:, :, :, 0:32], xn4[:, :, :, 32:64], s4[:, :, :, 0:32].to_broadcast([128, 10, 2, 32]),
                     ALU.mult, [xn, cs_], [t2])
                k.tt("pool", t24[:, :, :, 32:64], xn4[:, :, :, 0:32], s4[:, :, :, 32:64].to_broadcast([128, 10, 2, 32]),
                     ALU.mult, [xn, cs_], [t2])
                k.tt("dve", ob[:], t1[:], t2[:], ALU.add, [t1, t2], [ob])
            for g, (h0, nh) in enumerate(((0, 8), (8, 2))):
                bank = P.banks[g]
                bv = bank_bf(P, g).rearrange("p (a b) -> p a b", a=8)
                for j in range(nh):
                    k.tr(bv[:, j, :], ob[:, h0 + j, :], identb[:], [ob, identb], [bank])
                if g == 0:
                    evac(qT[:, :, rows], bv[:, :, :], [bank], [qT])
                else:
                    evac(kT[:, :, rows], bv[:, 0:2, :], [bank], [kT])
        P.release(m1)
        et3 = [P.alloc([512], BF16) for _ in range(3)]
        g2 = [P.alloc([512], F32) for _ in range(2)]
        rec2 = [P.alloc([512], F32) for _ in range(2)]
        o2 = [P.alloc([512], F32) for _ in range(2)]
        ob2 = [P.alloc([512], BF16) for _ in range(2)]
        groups = [(0, T_CTX, 0, NCT)] if not last else []
        groups += [(T_CTX + t0, tn, 0, NT) for (t0, tn) in tok_groups(T_LAT)]
        gi = 0
        ei = 0
        for h in range(8):
            kvh = h // 4
            for (q0, N, kt0, kt1) in groups:
                bo = P.banks[4 + 2 * (gi % 2)]
                bd = P.banks[5 + 2 * (gi % 2)]
                for kt in range(kt0, kt1):
                    bs = P.banks[2 + ei % 2]
                    et = et3[ei % 3]
                    ei += 1
                    k.mm(bs[:, 0:N], kT[:, kvh, kt * 128:(kt + 1) * 128], qT[:, h, q0:q0 + N], True, True, [kT, qT], [bs])
                    k.act(et[:, 0:N], bs[:, 0:N], AF.Exp, [bs], [et])
                    k.mm(bo[:, 0:N], Vb[:, kt, kvh * 128:(kvh + 1) * 128], et[:, 0:N], kt == kt0, kt == kt1 - 1, [Vb, et], [bo])
                    k.mm(bd[:, 0:N], onesb[:], et[:, 0:N], kt == kt0, kt == kt1 - 1, [onesb, et], [bd])
                gt, rec, o, obf = g2[gi % 2], rec2[gi % 2], o2[gi % 2], ob2[gi % 2]
                P.dma("sp", gt[:, 0:N], pFM[FM_AG + h * 128:FM_AG + (h + 1) * 128, q0:q0 + N], r=[R_pFM], w=[gt])
                k.act(gt[:, 0:N], gt[:, 0:N], AF.Silu, [gt], [gt])
                k.recip(rec[:, 0:N], bd[:, 0:N], [bd], [rec])
                k.tt("dve", o[:, 0:N], bo[:, 0:N], rec[:, 0:N], ALU.mult, [bo, rec], [o])
                k.tt("pool", obf[:, 0:N], o[:, 0:N], gt[:, 0:N], ALU.mult, [o, gt], [obf])
                P.dma("sp", yT[h * 128:(h + 1) * 128, q0:q0 + N], obf[:, 0:N], r=[obf], w=[R_yT])
                gi += 1
        P.release(m0)

    return run


def rope_table(t_lat):
    pos = np.arange(t_lat)
    inv = 10000.0 ** (-np.arange(0, 64, 2, dtype=np.float32) / 64).astype(np.float32)
    tab = np.zeros((t_lat, 2, 128), np.float32)
    for a, p in enumerate(((pos // 64).astype(np.float32), (pos % 64).astype(np.float32))):
        ang = (p[:, None] * inv[None, :]).astype(np.float32)
        c, s = np.cos(ang), np.sin(ang)
        tab[:, 0, a * 64:a * 64 + 32] = c
        tab[:, 0, a * 64 + 32:a * 64 + 64] = c
        tab[:, 1, a * 64:a * 64 + 32] = -s
        tab[:, 1, a * 64 + 32:a * 64 + 64] = s
    return tab


TWO_PI = 2.0 * math.pi


def hy_tables(L):
    N = 2 * L
    N1 = N // 128
    H1 = N1 // 2
    f = np.float64
    s1 = np.arange(H1, dtype=f)[:, None]
    f1 = np.arange(N1, dtype=f)[None, :]
    ang = TWO_PI * s1 * f1 / N1
    DA = np.zeros((N1, 4 * N1), f)
    for c in range(2):
        DA[c * H1:(c + 1) * H1, c * 2 * N1:c * 2 * N1 + N1] = np.cos(ang)
        DA[c * H1:(c + 1) * H1, c * 2 * N1 + N1:(c + 1) * 2 * N1] = -np.sin(ang)
    s2 = np.arange(128, dtype=f)[:, None]
    a1 = TWO_PI * s2 * np.arange(N1, dtype=f)[None, :] / N
    TW1 = np.stack([np.cos(a1), np.sin(a1), -np.sin(a1)], axis=1)
    a2 = TWO_PI * s2 * np.arange(128, dtype=f)[None, :] / 128
    C, S = np.cos(a2), np.sin(a2)
    DB = np.stack([C, S, -S], axis=1)
    IC = np.stack([np.concatenate([C, S], 1), np.concatenate([-S, C], 1)], axis=1)
    f1c = np.arange(N1, dtype=f)[:, None]
    a3 = TWO_PI * f1c * np.arange(128, dtype=f)[None, :] / N
    t2 = np.stack([np.cos(a3), -np.sin(a3), np.sin(a3)], axis=1)
    TW2 = np.concatenate([t2, t2], axis=0)
    a4 = TWO_PI * np.arange(N1, dtype=f)[:, None] * np.arange(H1, dtype=f)[None, :] / N1
    DD = np.zeros((2 * N1, 2, 2 * H1), f)
    for c in range(2):
        DD[c * N1:(c + 1) * N1, 0, c * H1:(c + 1) * H1] = np.cos(a4) / N
        DD[c * N1:(c + 1) * N1, 1, c * H1:(c + 1) * H1] = -np.sin(a4) / N
    pos = np.arange(L, dtype=np.float32)
    t = pos / max(L - 1, 1)
    bands = np.linspace(1e-4, 15, 16, dtype=np.float32)
    angz = (np.float32(TWO_PI / L) * pos[:, None] * bands).astype(np.float32)
    zT = np.concatenate([t[:, None], np.cos(angz), np.sin(angz)], axis=-1).T
    out = dict(DA=DA, TW1=TW1, DB=DB, IC=IC, TW2=TW2, DD=DD, zT=zT, trow=t)
    return {k_: np.ascontiguousarray(v, dtype=np.float32) for k_, v in out.items()}


def hy_delta_table():
    d = np.abs(np.linspace(math.log(1e-2) / 1.5, math.log(1e-2) / 0.3, 1024, dtype=np.float32))
    return np.ascontiguousarray(d.reshape(8, 128).T)


def make_hyena(env):
    P, k, cfg = env["P"], env["k"], env["cfg"]
    T, NT, NCT, T_CTX, T_LAT = env["T"], env["NT"], env["NCT"], env["T_CTX"], env["T_LAT"]
    pFM, yT = env["pFM"], env["yT"]
    R_pFM, R_yT = env["R_pFM"], env["R_yT"]
    inp = env["inp"]
    WD = cfg.get("wdepth", DEPTH)
    NG = cfg.get("hy_groups", 8)
    hy_cw = inp("hy_cw", [WD, 128, 3, 24])
    hy_cb = inp("hy_cb", [WD, 128, 24])
    hy_d = inp("hy_dp", [WD, 2, 2, 512])
    hy_w1 = inp("hy_w1", [WD, 33, 64])
    hy_b1 = inp("hy_b1", [WD, 64])
    hy_f1 = inp("hy_freq1", [WD, 64])
    hy_w2 = inp("hy_w2", [WD, 64, 64])
    hy_b2 = inp("hy_b2", [WD, 64])
    hy_f2 = inp("hy_freq2", [WD, 64])
    hy_w3 = inp("hy_w3", [WD, 64, 4096])
    hy_delta = inp("hy_delta", [128, 8])
    seqs = [(T_LAT, T_CTX, "l"), (T_CTX, 0, "c")]
    tabs = {}
    for (L, r0, tag) in seqs:
        N1 = 2 * L // 128
        tabs[tag] = dict(DA=inp("hy_DA_" + tag, [N1, 4 * N1]), TW1=inp("hy_TW1_" + tag, [128, 3, N1]),
                         DB=inp("hy_DB_" + tag, [128, 3, 128]), IC=inp("hy_IC_" + tag, [128, 2, 256]),
                         TW2=inp("hy_TW2_" + tag, [2 * N1, 3, 128]), DD=inp("hy_DD_" + tag, [2 * N1, 2, N1]),
                         zT=inp("hy_zT_" + tag, [33, L]), trow=inp("hy_trow_" + tag, [L]))
    hyS = P.dram("hyS", [3, 1024, T], F32, "Internal")
    hyK = {tag: P.dram("hyK_" + tag, [4, 1024, L], F32, "Internal") for (L, r0, tag) in seqs}
    R_hyS, R_hyK = Res(), Res()
    rb = {"i": 0}

    def nb():
        b = P.banks[rb["i"] % 8]
        rb["i"] += 1
        return b

    def cmul(A, O, t1, Tc, Tm, Tp, rA, rT, wO, wt1):
        p_, q_, _, n_ = O.shape
        bc4 = lambda ap: ap.unsqueeze(1).unsqueeze(1).to_broadcast([p_, q_, 2, n_])
        bc3 = lambda ap: ap.unsqueeze(1).to_broadcast([p_, q_, n_])
        k.tt("dve", t1, A, bc4(Tc), ALU.mult, rA + rT, wt1)
        k.tt("dve", O[:, :, 0, :], A[:, :, 1, :], bc3(Tm), ALU.mult, rA + rT, wO)
        k.tt("dve", O[:, :, 1, :], A[:, :, 0, :], bc3(Tp), ALU.mult, rA + rT, wO)
        k.tt("pool", O, O, t1, ALU.add, wO + wt1, wO)

    def run(l, last):
        m0 = P.mark()
        cw = P.alloc([3, 24], F32)
        cb = P.alloc([24], F32)
        P.dma("sp", cw[:], hy_cw[l], w=[cw])
        P.dma("sp", cb[:], hy_cb[l], w=[cb])
        xi2 = [P.alloc([T], F32) for _ in range(2)]
        xo2 = [P.alloc([T], F32) for _ in range(2)]
        segs = [(0, T_CTX), (T_CTX, T)] if not last else [(T_CTX, T)]
        ci = 0
        for j in range(3):
            for g in range(NG):
                xi, xo = xi2[ci % 2], xo2[ci % 2]
                ci += 1
                jj = j * 8 + g
                P.dma("sp", xi[:], pFM[FM_HV + j * 1024 + g * 128:FM_HV + j * 1024 + (g + 1) * 128, :], r=[R_pFM], w=[xi])
                k.ts("dve", xo[:], xi[:], cw[:, 1, jj:jj + 1], cb[:, jj:jj + 1], ALU.mult, ALU.add, [xi, cw, cb], [xo])
                for (a, b) in segs:
                    k.stt(xo[:, a + 1:b], xi[:, a:b - 1], cw[:, 0, jj:jj + 1], xo[:, a + 1:b], ALU.mult, ALU.add, [xi, cw, xo], [xo])
                    k.stt(xo[:, a:b - 1], xi[:, a + 1:b], cw[:, 2, jj:jj + 1], xo[:, a:b - 1], ALU.mult, ALU.add, [xi, cw, xo], [xo])
                P.dma("sp", hyS[j, g * 128:(g + 1) * 128, :], xo[:], r=[xo], w=[R_hyS])
        P.release(m0)
        for (L, r0, tag) in seqs:
            if last and tag == "c":
                continue
            m1 = P.mark()
            tb = tabs[tag]
            zT = P.alloc([L], F32)
            trow = P.alloc([L], F32)
            h1 = P.alloc([L], F32)
            h2 = P.alloc([L], F32)
            w1 = P.alloc([64], F32)
            w2 = P.alloc([64], F32)
            w3 = P.alloc([4096], F32)
            col = P.alloc([8], F32)
            dl = P.alloc([8], F32)
            ndl = P.alloc([8], F32)
            P.dma("sp", zT[0:33, :], tb["zT"][:, :], w=[zT])
            P.dma("sp", trow[:], tb["trow"].partition_broadcast(128), w=[trow])
            P.dma("sp", w1[0:33, :], hy_w1[l], w=[w1])
            P.dma("sp", w2[0:64, :], hy_w2[l], w=[w2])
            P.dma("sp", w3[0:64, :], hy_w3[l], w=[w3])
            for i_, src in enumerate((hy_b1, hy_f1, hy_b2, hy_f2)):
                P.dma("sp", col[0:64, i_:i_ + 1], src[l].rearrange("(p o) -> p o", o=1), w=[col])
            P.dma("sp", dl[:], hy_delta[:, :], w=[dl])
            k.ts("dve", ndl[:], dl[:], -1.0, None, ALU.mult, None, [dl], [ndl])
            k.tt("dve", col[0:64, 4:5], col[0:64, 0:1], col[0:64, 1:2], ALU.mult, [col], [col])
            k.tt("dve", col[0:64, 5:6], col[0:64, 2:3], col[0:64, 3:4], ALU.mult, [col], [col])
            ua = P.alloc([512], F32)
            ub = P.alloc([512], F32)
            for (src, wt, kk_, fr, fb, dst) in ((zT, w1, 33, 1, 4, h1), (h1, w2, 64, 3, 5, h2)):
                for (c0, cn) in tok_groups(L):
                    bank = nb()
                    k.mm(bank[0:64, 0:cn], wt[0:kk_, 0:64], src[0:kk_, c0:c0 + cn], True, True, [wt, src], [bank])
                    k.act(ua[0:64, 0:cn], bank[0:64, 0:cn], AF.Identity, [bank, col], [ua], bias=col[0:64, fb:fb + 1],
                          scale=col[0:64, fr:fr + 1])
                    k.ts("dve", ub[0:64, 0:cn], ua[0:64, 0:cn], 1.0 / TWO_PI, 12582912.0, ALU.mult, ALU.add, [ua], [ub])
                    k.ts("dve", ub[0:64, 0:cn], ub[0:64, 0:cn], -12582912.0, None, ALU.add, None, [ub], [ub])
                    k.stt(ua[0:64, 0:cn], ub[0:64, 0:cn], -TWO_PI, ua[0:64, 0:cn], ALU.mult, ALU.add, [ub, ua], [ua])
                    k.ts("dve", ua[0:64, 0:cn], ua[0:64, 0:cn], 3.141592, -3.141592, ALU.min, ALU.max, [ua], [ua])
                    k.act(dst[0:64, c0:c0 + cn], ua[0:64, 0:cn], AF.Sin, [ua], [dst])
            dec = P.alloc([L], F32)
            kf = [P.alloc([L], F32) for _ in range(2)]
            nr = P.alloc([4], F32)
            for g in range(NG):
                k.act(dec[:], trow[:], AF.Exp, [trow, ndl], [dec], scale=ndl[:, g:g + 1])
                for order in range(2):
                    for dr in range(2):
                        c00 = dr * 2048 + order * 1024 + g * 128
                        for (c0, cn) in tok_groups(L):
                            bank = nb()
                            k.mm(bank[:, 0:cn], w3[0:64, c00:c00 + 128], h2[0:64, c0:c0 + cn], True, True, [w3, h2], [bank])
                            k.tt("dve", kf[dr][:, c0:c0 + cn], bank[:, 0:cn], dec[:, c0:c0 + cn], ALU.mult, [bank, dec], [kf[dr]])
                    k.memset("dve", kf[1][:, 0:1], 0.0, [kf[1]])
                    for dr in range(2):
                        P.op("dve", lambda e, o=nr[:, dr:dr + 1], i=kf[dr][:]: e.tensor_reduce(
                            out=o, in_=i, axis=AX.X, op=ALU.add, apply_absolute_value=True), [kf[dr]], [nr])
                    k.stt(nr[:, 2:3], nr[:, 0:1], 1e-6, nr[:, 1:2], ALU.add, ALU.add, [nr], [nr])
                    k.recip(nr[:, 3:4], nr[:, 2:3], [nr], [nr])
                    for dr in range(2):
                        k.ts("dve", kf[dr][:], kf[dr][:], nr[:, 3:4], None, ALU.mult, None, [kf[dr], nr], [kf[dr]])
                        P.dma("sp", hyK[tag][dr * 2 + order, g * 128:(g + 1) * 128, :], kf[dr][:], r=[kf[dr]], w=[R_hyK])
            P.release(m1)
        NB = 16
        for (L, r0, tag) in seqs:
            if last and tag == "c":
                continue
            m1 = P.mark()
            tb = tabs[tag]
            N1 = 2 * L // 128
            H1 = N1 // 2
            P2 = 2 * N1
            DA = P.alloc([4 * N1], F32)
            TW1 = P.alloc([3, N1], F32)
            DB = P.alloc([3, 128], F32)
            IC = P.alloc([2, 256], F32)
            TW2 = P.alloc([3, 128], F32)
            DD = P.alloc([2, N1], F32)
            P.dma("sp", DA[0:N1, :], tb["DA"][:, :], w=[DA])
            P.dma("sp", TW1[:], tb["TW1"][:, :, :], w=[TW1])
            P.dma("sp", DB[:], tb["DB"][:, :, :], w=[DB])
            P.dma("sp", IC[:], tb["IC"][:, :, :], w=[IC])
            P.dma("sp", TW2[0:P2, :, :], tb["TW2"][:, :, :], w=[TW2])
            P.dma("sp", DD[0:P2, :, :], tb["DD"][:, :, :], w=[DD])
            Kt = P.alloc([128, 2, N1], F32)
            zg = P.alloc([64, 128], F32)
            Ap = P.alloc([NB, 2, N1], F32)
            Yt = P.alloc([2, NB, N1], F32)
            Ytf = Yt[:].rearrange("p r q n -> p r (q n)")
            Bp = P.alloc([NB // 2, 2, 128], F32)
            ct1 = P.alloc([4 * 2 * max(N1, 64)], F32)
            pq = [P.alloc([8, N1], F32) for _ in range(4)]
            Xa = [P.alloc([NB // 2, 128], F32) for _ in range(2)]
            Xb = [P.alloc([NB // 2, 128], F32) for _ in range(2)]
            dT = P.alloc([NB // 2], F32)
            tt_ = P.alloc([4, 128], F32)
            ob = P.alloc([NB // 2, 128], BF16)

            def load_T(dst, src2d, c0, rres):
                for c in range(2):
                    P.dma("sp", dst[c * H1:(c + 1) * H1, :, :],
                          src2d[c0 + c:c0 + NB:2, :].rearrange("q (a b) -> a q b", b=128), r=[rres], w=[dst])

            def fft_fwd(X, XR, consume):
                for pb in range(NB // 4):
                    bank = nb()
                    for pp in range(2):
                        k.mm(bank[:, pp * 4 * N1:(pp + 1) * 4 * N1], X[0:N1, pb * 2 + pp, :], DA[0:N1, :], True, True, [XR, DA], [bank])
                    A = bank[:, 0:8 * N1].rearrange("p (q r n) -> p q r n", q=4, r=2)
                    cmul(A, Ap[:, pb * 4:(pb + 1) * 4, :, :], ct1[:, 0:8 * N1].rearrange("p (q r n) -> p q r n", q=4, r=2),
                         TW1[:, 0, :], TW1[:, 1, :], TW1[:, 2, :], [bank], [TW1], [Ap], [ct1])
                for cg in range(NB // 8):
                    bre, bim = nb(), nb()
                    cs_ = slice(cg * 8, (cg + 1) * 8)
                    ore = bre[:, 0:8 * N1].rearrange("p (q n) -> p q n", q=8)
                    oim = bim[:, 0:8 * N1].rearrange("p (q n) -> p q n", q=8)
                    fre, fim = bre[:, 0:8 * N1], bim[:, 0:8 * N1]
                    k.mm(fre, DB[:, 0, :], Ap[:, cs_, 0, :], True, False, [DB, Ap], [bre])
                    k.mm(fre, DB[:, 1, :], Ap[:, cs_, 1, :], False, True, [DB, Ap], [bre])
                    k.mm(fim, DB[:, 0, :], Ap[:, cs_, 1, :], True, False, [DB, Ap], [bim])
                    k.mm(fim, DB[:, 2, :], Ap[:, cs_, 0, :], False, True, [DB, Ap], [bim])
                    consume(cg, bre, bim, ore, oim)

            def conv(X, XR, ch0, epilogue):
                def consume(cg, bre, bim, ore, oim):
                    kc_ = slice(ch0 + cg * 8, ch0 + (cg + 1) * 8)
                    ys = slice(cg * 8, (cg + 1) * 8)
                    k.tt("dve", pq[0][:], ore, Kt[:, kc_, 0, :], ALU.mult, [bre, Kt], [pq[0]])
                    k.tt("dve", pq[1][:], oim, Kt[:, kc_, 1, :], ALU.mult, [bim, Kt], [pq[1]])
                    k.tt("pool", Yt[:, 0, ys, :], pq[0][:], pq[1][:], ALU.subtract, [pq[0], pq[1]], [Yt])
                    k.tt("dve", pq[2][:], ore, Kt[:, kc_, 1, :], ALU.mult, [bre, Kt], [pq[2]])
                    k.tt("dve", pq[3][:], oim, Kt[:, kc_, 0, :], ALU.mult, [bim, Kt], [pq[3]])
                    k.tt("pool", Yt[:, 1, ys, :], pq[2][:], pq[3][:], ALU.add, [pq[2], pq[3]], [Yt])
                fft_fwd(X, XR, consume)
                for pb in range(NB // 4):
                    bank = nb()
                    for pp in range(2):
                        pr = pb * 2 + pp
                        o_ = bank[0:P2, pp * 256:(pp + 1) * 256]
                        k.mm(o_, Ytf[:, 0, 2 * pr * N1:(2 * pr + 2) * N1], IC[:, 0, :], True, False, [Yt, IC], [bank])
                        k.mm(o_, Ytf[:, 1, 2 * pr * N1:(2 * pr + 2) * N1], IC[:, 1, :], False, True, [Yt, IC], [bank])
                    A = bank[0:P2, :].rearrange("p (q r n) -> p q r n", q=2, r=2)
                    cmul(A, Bp[0:P2, pb * 2:(pb + 1) * 2, :, :], ct1[0:P2, 0:512].rearrange("p (q r n) -> p q r n", q=2, r=2),
                         TW2[0:P2, 0, :], TW2[0:P2, 1, :], TW2[0:P2, 2, :], [bank], [TW2], [Bp], [ct1])
                for pg in range(NB // 8):
                    bank = nb()
                    ps_ = slice(pg * 4, (pg + 1) * 4)
                    o_ = bank[0:N1, :].rearrange("p (q n) -> p q n", q=4)
                    k.mm(bank[0:N1, :], DD[0:P2, 0, :], Bp[0:P2, ps_, 0, :], True, False, [DD, Bp], [bank])
                    k.mm(bank[0:N1, :], DD[0:P2, 1, :], Bp[0:P2, ps_, 1, :], False, True, [DD, Bp], [bank])
                    epilogue(pg, bank, o_)

            def load_d(order, ch0):
                for c in range(2):
                    P.dma("sp", dT[c * H1:(c + 1) * H1, :], hy_d[l, order, c, ch0 // 2:ch0 // 2 + NB // 2].partition_broadcast(H1), w=[dT])

            for g in range(NG):
                for order in range(2):
                    for dr in range(2):
                        for sb in range(128 // NB):
                            X = Xa[sb % 2]
                            load_T(X, hyK[tag][dr * 2 + order], g * 128 + sb * NB, R_hyK)

                            def consume(cg, bre, bim, ore, oim, dr=dr, sb=sb):
                                kc_ = slice(sb * NB + cg * 8, sb * NB + (cg + 1) * 8)
                                if dr == 0:
                                    k.copy("act", Kt[:, kc_, 0, :], ore, [bre], [Kt])
                                    k.copy("dve", Kt[:, kc_, 1, :], oim, [bim], [Kt])
                                else:
                                    k.tt("dve", Kt[:, kc_, 0, :], Kt[:, kc_, 0, :], ore, ALU.add, [Kt, bre], [Kt])
                                    k.tt("dve", Kt[:, kc_, 1, :], Kt[:, kc_, 1, :], oim, ALU.subtract, [Kt, bim], [Kt])
                            fft_fwd(X, X, consume)
                    for sb in range(128 // NB):
                        ch0 = g * 128 + sb * NB
                        zs = zg[0:N1, sb * (NB // 2):(sb + 1) * (NB // 2), :]
                        bcd = lambda: dT[0:N1, :].unsqueeze(2).to_broadcast([N1, NB // 2, 128])
                        load_d(order, ch0)
                        if order == 0:
                            Xv, Xx = Xa[sb % 2], Xb[sb % 2]
                            load_T(Xv, hyS[0][:, r0:r0 + L], ch0, R_hyS)
                            load_T(Xx, hyS[1][:, r0:r0 + L], ch0, R_hyS)

                            def epi(pg, bank, o_, Xv=Xv, Xx=Xx, zs=zs):
                                ps_ = slice(pg * 4, (pg + 1) * 4)
                                dv = dT[0:N1, ps_].unsqueeze(2).to_broadcast([N1, 4, 128])
                                k.tt("pool", tt_[0:N1], Xv[0:N1, ps_, :], dv, ALU.mult, [Xv, dT], [tt_])
                                k.tt("dve", tt_[0:N1], tt_[0:N1], o_, ALU.add, [tt_, bank], [tt_])
                                k.tt("pool", zs[:, ps_, :], tt_[0:N1], Xx[0:N1, ps_, :], ALU.mult, [tt_, Xx], [zg])
                            conv(Xv, Xv, sb * NB, epi)
                        else:
                            Xx, Xg = Xa[sb % 2], Xb[sb % 2]
                            load_T(Xx, hyS[2][:, r0:r0 + L], ch0, R_hyS)
                            load_T(Xg, pFM[FM_HG:FM_HG + 1024, r0:r0 + L], ch0, R_pFM)
                            k.act(Xg[0:N1], Xg[0:N1], AF.Silu, [Xg], [Xg])

                            def epi(pg, bank, o_, Xx=Xx, Xg=Xg, zs=zs):
                                ps_ = slice(pg * 4, (pg + 1) * 4)
                                dv = dT[0:N1, ps_].unsqueeze(2).to_broadcast([N1, 4, 128])
                                k.tt("pool", tt_[0:N1], zs[:, ps_, :], dv, ALU.mult, [zg, dT], [tt_])
                                k.tt("dve", tt_[0:N1], tt_[0:N1], o_, ALU.add, [tt_, bank], [tt_])
                                k.tt("pool", tt_[0:N1], tt_[0:N1], Xx[0:N1, ps_, :], ALU.mult, [tt_, Xx], [tt_])
                                k.tt("pool", ob[0:N1, ps_, :], tt_[0:N1], Xg[0:N1, ps_, :], ALU.mult, [tt_, Xg], [ob])
                            conv(zs, zg, sb * NB, epi)
                            for c in range(2):
                                P.dma("sp", yT[1024 + ch0 + c:1024 + ch0 + NB:2, r0:r0 + L].rearrange("q (a b) -> a q b", b=128),
                                      ob[c * H1:(c + 1) * H1, :, :], r=[ob], w=[R_yT])
            P.release(m1)
        P.release(m0)

    return run


def make_deltanet(env):
    P, k, cfg = env["P"], env["k"], env["cfg"]
    T, NT, NCT, T_CTX, T_LAT = env["T"], env["NT"], env["NCT"], env["T_CTX"], env["T_LAT"]
    pTM, pFM, yT = env["pTM"], env["pFM"], env["yT"]
    R_pTM, R_pFM, R_yT = env["R_pTM"], env["R_pFM"], env["R_yT"]
    identf, identb, evac, inp = env["identf"], env["identb"], env["evac"], env["inp"]
    WD = cfg.get("wdepth", DEPTH)
    NJ = cfg.get("dn_heads", 4)
    dn_cw = inp("dn_cw", [WD, 128, 3, 16])
    dn_alog = inp("dn_a_log", [WD, 16])
    dn_dtb = inp("dn_dt_bias", [WD, 16])
    dn_ng = inp("dn_norm_g", [WD, 128])
    dn_masks = inp("dn_masks", [128, 5, 128])
    dnS = P.dram("dnS", [16, 128, T], F32, "Internal")
    R_dnS = Res()
    rb = {"i": 0}

    def nb():
        b = P.banks[rb["i"] % 8]
        rb["i"] += 1
        return b

    def run(l, last):
        m0 = P.mark()
        cw = P.alloc([3, 16], F32)
        ones = P.alloc([128], F32)
        P.dma("sp", cw[:], dn_cw[l], w=[cw])
        k.memset("dve", ones[:], 1.0, [ones])
        xi2 = [P.alloc([T], F32) for _ in range(2)]
        xo2 = [P.alloc([T], F32) for _ in range(2)]
        sq = P.alloc([512], F32)
        rt = P.alloc([512], F32)
        segs = [(0, T_CTX), (T_CTX, T)]
        for grp in range(16):
            if (grp < 8 and grp % 4 >= NJ) or (grp >= 8 and (grp - 8) // 2 >= NJ):
                continue
            xi, xo = xi2[grp % 2], xo2[grp % 2]
            P.dma("sp", xi[:], pFM[FM_DQ + grp * 128:FM_DQ + (grp + 1) * 128, :], r=[R_pFM], w=[xi])
            k.ts("dve", xo[:], xi[:], cw[:, 1, grp:grp + 1], None, ALU.mult, None, [xi, cw], [xo])
            for (a, b) in segs:
                k.stt(xo[:, a + 1:b], xi[:, a:b - 1], cw[:, 0, grp:grp + 1], xo[:, a + 1:b], ALU.mult, ALU.add, [xi, cw, xo], [xo])
                k.stt(xo[:, a:b - 1], xi[:, a + 1:b], cw[:, 2, grp:grp + 1], xo[:, a:b - 1], ALU.mult, ALU.add, [xi, cw, xo], [xo])
            k.act(xo[:], xo[:], AF.Silu, [xo], [xo])
            if grp < 8:
                for (c0, cn) in tok_groups(T):
                    bank = nb()
                    k.tt("pool", sq[:, 0:cn], xo[:, c0:c0 + cn], xo[:, c0:c0 + cn], ALU.mult, [xo], [sq])
                    k.mm(bank[:, 0:cn], ones[:], sq[:, 0:cn], True, True, [ones, sq], [bank])
                    k.ts("dve", rt[:, 0:cn], bank[:, 0:cn], 1e-6, None, ALU.add, None, [bank], [rt])
                    k.act(rt[:, 0:cn], rt[:, 0:cn], AF.Sqrt, [rt], [rt])
                    k.recip(rt[:, 0:cn], rt[:, 0:cn], [rt], [rt])
                    if grp < 4:
                        k.stt(xo[:, c0:c0 + cn], xo[:, c0:c0 + cn], 128.0 ** -0.5, rt[:, 0:cn], ALU.mult, ALU.mult, [xo, rt], [xo])
                    else:
                        k.tt("dve", xo[:, c0:c0 + cn], xo[:, c0:c0 + cn], rt[:, 0:cn], ALU.mult, [xo, rt], [xo])
            P.dma("sp", dnS[grp], xo[:], r=[xo], w=[R_dnS])
        P.release(m0)
        if cfg.get("dn_stop", 9) <= 1:
            return
        masks = P.alloc([5, 128], F32)
        ones = P.alloc([128], F32)
        ngb = P.alloc([128], F32)
        alb = P.alloc([16], F32)
        dtb = P.alloc([16], F32)
        P.dma("sp", masks[:], dn_masks[:, :, :], w=[masks])
        P.dma("sp", ngb[:], dn_ng[l].partition_broadcast(128), w=[ngb])
        P.dma("sp", alb[:], dn_alog[l].partition_broadcast(128), w=[alb])
        P.dma("sp", dtb[:], dn_dtb[l].partition_broadcast(128), w=[dtb])
        k.memset("dve", ones[:], 1.0, [ones])
        k.act(alb[:], alb[:], AF.Exp, [alb], [alb])
        k.ts("dve", alb[:], alb[:], -1.0, None, ALU.mult, None, [alb], [alb])
        ab = P.alloc([NT, 32], F32)
        P.dma("sp", ab[:], pTM[:, TM_A:TM_A + 32].rearrange("(n p) c -> p n c", p=128), r=[R_pTM], w=[ab])
        gall = P.alloc([NT, 16], F32)
        ball = P.alloc([NT, 16], F32)
        k.tt("dve", gall[:], ab[:, :, 0:16], dtb[:].unsqueeze(1).to_broadcast([128, NT, 16]), ALU.add, [ab, dtb], [gall])
        k.act(gall[:], gall[:], AF.Exp, [gall], [gall])
        k.act(gall[:], gall[:], AF.Ln, [gall], [gall], bias=1.0)
        k.tt("dve", gall[:], gall[:], alb[:].unsqueeze(1).to_broadcast([128, NT, 16]), ALU.mult, [gall, alb], [gall])
        k.act(ball[:], ab[:, :, 16:32], AF.Sigmoid, [ab], [ball])
        m1 = P.mark()
        if cfg.get("dn_stop", 9) <= 2:
            return
        for j in range(NJ):
            m1 = P.mark()
            qT = P.alloc([T], F32)
            kT = P.alloc([T], F32)
            vT = [P.alloc([T], F32) for _ in range(2)]
            P.dma("sp", qT[:], dnS[j], r=[R_dnS], w=[qT])
            P.dma("sp", kT[:], dnS[4 + j], r=[R_dnS], w=[kT])
            for hh in range(2):
                P.dma("sp", vT[hh][:], dnS[8 + 2 * j + hh], r=[R_dnS], w=[vT[hh]])
            oacc = P.alloc([NT, 2, 128], F32)
            k.memset("pool", oacc[:], 0.0, [oacc])
            ktok = P.alloc([128], F32)
            kk = P.alloc([128], F32)
            qkT = P.alloc([128], F32)
            sq_ = lambda: P.alloc([128], F32)
            hd = []
            for hh in range(2):
                hd.append(dict(S=sq_(), vtok=sq_(), grep=sq_(), brep=sq_(), gcr=sq_(), e1=sq_(), DT=sq_(), QKT=sq_(), tM=sq_(),
                               M=sq_(), Rm=sq_(), Pa=sq_(), Pb=sq_(), Qa=sq_(), Qb=sq_(), Er=sq_(), qdT=sq_(), vb=sq_(), kbe=sq_(),
                               kdec=sq_(), u=sq_(), wT=sq_(), vnew=sq_(), cols=P.alloc([8], F32)))

            def chain(hh, dr, n):
                d = hd[hh]
                h = 2 * j + hh
                ci = dr * 8 + h
                gcol = gall[:, n, ci:ci + 1]
                bcol = ball[:, n, ci:ci + 1]
                Mk, Ms, Id = masks[:, 2 * dr, :], masks[:, 2 * dr + 1, :], masks[:, 4, :]
                cs_ = slice(n * 128, (n + 1) * 128)
                cols = d["cols"]
                k.ts("pool", d["grep"][:], ones[:], gcol, None, ALU.mult, None, [ones, gall], [d["grep"]])
                k.ts("pool", d["brep"][:], ones[:], bcol, None, ALU.mult, None, [ones, ball], [d["brep"]])
                b1, b2, b3 = nb(), nb(), nb()
                k.mm(b1[:, 0:128], d["grep"][:], Mk, True, True, [d["grep"], masks], [b1])
                k.mm(b2[:, 0:1], Mk, gcol, True, True, [masks, gall], [b2])
                k.mm(b3[:, 0:128], d["brep"][:], Id, True, True, [d["brep"], masks], [b3])
                yield
                k.copy("dve", cols[:, 0:1], b2[:, 0:1], [b2], [cols])
                k.copy("act", d["gcr"][:], b1[:, 0:128], [b1], [d["gcr"]])
                gcl = d["gcr"][:, 127:128] if dr == 0 else d["gcr"][:, 0:1]
                k.ts("dve", d["e1"][:], d["gcr"][:], cols[:, 0:1], 0.0, ALU.subtract, ALU.min, [d["gcr"], cols], [d["e1"]])
                k.act(d["e1"][:], d["e1"][:], AF.Exp, [d["e1"]], [d["e1"]])
                k.tt("pool", d["DT"][:], d["e1"][:], Mk, ALU.mult, [d["e1"], masks], [d["DT"]])
                k.tt("pool", d["QKT"][:], qkT[:], d["DT"][:], ALU.mult, [qkT, d["DT"]], [d["QKT"]])
                k.tt("pool", d["tM"][:], d["DT"][:], Ms, ALU.mult, [d["DT"], masks], [d["tM"]])
                k.tt("pool", d["tM"][:], d["tM"][:], kk[:], ALU.mult, [d["tM"], kk], [d["tM"]])
                k.tt("dve", d["M"][:], d["tM"][:], b3[:, 0:128], ALU.mult, [d["tM"], b3], [d["M"]])
                k.act(cols[:, 1:2], cols[:, 0:1], AF.Exp, [cols], [cols])
                k.ts("dve", cols[:, 2:3], gcl, cols[:, 0:1], None, ALU.subtract, None, [d["gcr"], cols], [cols])
                k.act(cols[:, 2:3], cols[:, 2:3], AF.Exp, [cols], [cols])
                k.act(cols[:, 3:4], gcl, AF.Exp, [d["gcr"]], [cols])
                k.act(d["Er"][:], d["gcr"][:], AF.Exp, [d["gcr"]], [d["Er"]])
                k.tt("pool", d["qdT"][:], qT[:, cs_], d["Er"][:], ALU.mult, [qT, d["Er"]], [d["qdT"]])
                k.ts("dve", d["vb"][:], d["vtok"][:], bcol, None, ALU.mult, None, [d["vtok"], ball], [d["vb"]])
                k.ts("dve", d["kbe"][:], ktok[:], bcol, cols[:, 1:2], ALU.mult, ALU.mult, [ktok, ball, cols], [d["kbe"]])
                k.ts("dve", d["kdec"][:], ktok[:], cols[:, 2:3], None, ALU.mult, None, [ktok, cols], [d["kdec"]])
                k.tt("pool", d["Rm"][:], Id, d["M"][:], ALU.subtract, [masks, d["M"]], [d["Rm"]])
                bq = nb()
                k.tr(bq[:, 0:128], d["M"][:], identf[:], [d["M"], identf], [bq])
                yield
                Pc, Qc, Pn, Qn = d["M"], d["Qa"], d["Pa"], d["Qb"]
                k.copy("act", Qc[:], bq[:, 0:128], [bq], [Qc])
                for lev in range(1, 7):
                    bq2 = nb()
                    k.mm(bq2[:, 0:128], Pc[:], Qc[:], True, True, [Pc, Qc], [bq2])
                    if lev < 6:
                        bp2 = nb()
                        k.mm(bp2[:, 0:128], Qc[:], Pc[:], True, True, [Pc, Qc], [bp2])
                    yield
                    k.copy("act", Qn[:], bq2[:, 0:128], [bq2], [Qn])
                    if lev < 6:
                        k.copy("dve", Pn[:], bp2[:, 0:128], [bp2], [Pn])
                    br_ = nb()
                    k.mm(br_[:, 0:128], Qn[:], d["Rm"][:], True, True, [Qn, d["Rm"]], [br_])
                    yield
                    k.tt("dve", d["Rm"][:], d["Rm"][:], br_[:, 0:128], ALU.add, [d["Rm"], br_], [d["Rm"]])
                    Pc, Qc = Pn, Qn
                    Pn = d["Pb"] if Pc is d["Pa"] else d["Pa"]
                    Qn = d["Qa"] if Qc is d["Qb"] else d["Qb"]
                bu, bw = nb(), nb()
                k.mm(bu[:, 0:128], d["Rm"][:], d["vb"][:], True, True, [d["Rm"], d["vb"]], [bu])
                k.mm(bw[:, 0:128], d["kbe"][:], d["Rm"][:], True, True, [d["Rm"], d["kbe"]], [bw])
                yield
                k.copy("act", d["u"][:], bu[:, 0:128], [bu], [d["u"]])
                k.copy("dve", d["wT"][:], bw[:, 0:128], [bw], [d["wT"]])
                bv = nb()
                k.mm(bv[:, 0:128], d["wT"][:], d["S"][:], True, True, [d["wT"], d["S"]], [bv])
                yield
                k.tt("dve", d["vnew"][:], d["u"][:], bv[:, 0:128], ALU.subtract, [d["u"], bv], [d["vnew"]])
                bo, bs = nb(), nb()
                k.mm(bo[:, 0:128], d["qdT"][:], d["S"][:], True, False, [d["qdT"], d["S"]], [bo])
                k.mm(bo[:, 0:128], d["QKT"][:], d["vnew"][:], False, True, [d["QKT"], d["vnew"]], [bo])
                k.mm(bs[:, 0:128], d["kdec"][:], d["vnew"][:], True, True, [d["kdec"], d["vnew"]], [bs])
                yield
                k.tt("dve", oacc[:, n, hh, :], oacc[:, n, hh, :], bo[:, 0:128], ALU.add, [oacc, bo], [oacc])
                k.stt(d["S"][:], d["S"][:], cols[:, 3:4], bs[:, 0:128], ALU.mult, ALU.add, [d["S"], cols, bs], [d["S"]])

            for dr in range(2):
                order = list(range(NT)) if dr == 0 else (list(range(NCT - 1, -1, -1)) + list(range(NT - 1, NCT - 1, -1)))
                for hh in range(2):
                    k.memset("pool", hd[hh]["S"][:], 0.0, [hd[hh]["S"]])
                for n in order:
                    cs_ = slice(n * 128, (n + 1) * 128)
                    bt = nb()
                    k.tr(bt[:, 0:128], kT[:, cs_], identf[:], [kT, identf], [bt])
                    k.copy("act", ktok[:], bt[:, 0:128], [bt], [ktok])
                    for hh in range(2):
                        bt2 = nb()
                        k.tr(bt2[:, 0:128], vT[hh][:, cs_], identf[:], [vT[hh], identf], [bt2])
                        k.copy("dve", hd[hh]["vtok"][:], bt2[:, 0:128], [bt2], [hd[hh]["vtok"]])
                    bk, bqk = nb(), nb()
                    k.mm(bk[:, 0:128], kT[:, cs_], kT[:, cs_], True, True, [kT], [bk])
                    k.mm(bqk[:, 0:128], kT[:, cs_], qT[:, cs_], True, True, [kT, qT], [bqk])
                    k.copy("act", kk[:], bk[:, 0:128], [bk], [kk])
                    k.copy("dve", qkT[:], bqk[:, 0:128], [bqk], [qkT])
                    gens = [chain(hh, dr, n) for hh in range(2)]
                    alive = [True, True]
                    nseg = 0
                    while any(alive):
                        nseg += 1
                        if nseg > cfg.get("dn_segs", 999):
                            break
                        for hh in range(2):
                            if alive[hh]:
                                try:
                                    next(gens[hh])
                                except StopIteration:
                                    alive[hh] = False
            zt2 = [P.alloc([256], F32) for _ in range(2)]
            yn = P.alloc([128], F32)
            yb = P.alloc([2, 128], BF16)
            yo2 = [P.alloc([2, 128], BF16) for _ in range(2)]
            st = P.alloc([2], F32)
            for t in range(NT):
                zt = zt2[t % 2]
                P.dma("sp", zt[:], pTM[t * 128:(t + 1) * 128, TM_Z + 2 * j * 128:TM_Z + (2 * j + 2) * 128], r=[R_pTM], w=[zt])
                k.act(zt[:], zt[:], AF.Silu, [zt], [zt])
                for hh in range(2):
                    k.ssq(yn[:], oacc[:, t, hh, :], st[:, 0:1], [oacc], [yn, st])
                    k.ts("dve", st[:, 0:1], st[:, 0:1], 1.0 / 128, EPS, ALU.mult, ALU.add, [st], [st])
                    k.act(st[:, 0:1], st[:, 0:1], AF.Sqrt, [st], [st])
                    k.recip(st[:, 1:2], st[:, 0:1], [st], [st])
                    k.stt(yn[:], oacc[:, t, hh, :], st[:, 1:2], ngb[:], ALU.mult, ALU.mult, [oacc, st, ngb], [yn])
                    k.tt("pool", yb[:, hh, :], yn[:], zt[:, hh * 128:(hh + 1) * 128], ALU.mult, [yn, zt], [yb])
                bank = P.banks[t % 2]
                bv_ = bank_bf(P, t % 2).rearrange("p (a b) -> p a b", a=8)
                for hh in range(2):
                    k.tr(bv_[:, hh, :], yb[:, hh, :], identb[:], [yb, identb], [bank])
                yo = yo2[t % 2]
                evac(yo[:], bv_[:, 0:2, :], [bank], [yo])
                P.dma("sp", yT[2048 + 2 * j * 128:2048 + (2 * j + 2) * 128, t * 128:(t + 1) * 128].rearrange("(a p) t -> p a t", p=128),
                      yo[:], r=[yo], w=[R_yT])
            P.release(m1)
        P.release(m0)

    return run


def dn_mask_table():
    s = np.arange(128)[:, None]
    c = np.arange(128)[None, :]
    m = np.stack([s <= c, s < c, s >= c, s > c, s == c], axis=1).astype(np.float32)
    return np.ascontiguousarray(m)


def deltanet_inputs(w):
    f32 = np.float32
    cwf = np.asarray(w["dn_conv_w"], f32)
    WD = cwf.shape[0]
    cw = cwf.reshape(WD, 3, 16, 128).transpose(0, 3, 1, 2)
    return dict(dn_cw=np.ascontiguousarray(cw), dn_a_log=np.ascontiguousarray(np.asarray(w["dn_a_log"], f32).reshape(WD, 16)),
                dn_dt_bias=np.ascontiguousarray(np.asarray(w["dn_dt_bias"], f32).reshape(WD, 16)),
                dn_norm_g=np.ascontiguousarray(np.asarray(w["dn_norm_g"], f32)), dn_masks=dn_mask_table())


def hyena_inputs(w, t_ctx, t_lat):
    f32 = np.float32
    cwf = np.asarray(w["hy_conv_w"], f32)
    WD = cwf.shape[0]
    cw = cwf.reshape(WD, 3, 24, 128).transpose(0, 3, 1, 2)
    cb = np.asarray(w["hy_conv_b"], f32).reshape(WD, 24, 128).transpose(0, 2, 1)
    out = dict(hy_cw=np.ascontiguousarray(cw), hy_cb=np.ascontiguousarray(cb), hy_delta=hy_delta_table())
    out["hy_dp"] = np.ascontiguousarray(np.asarray(w["hy_d"], f32).reshape(WD, 2, 512, 2).transpose(0, 1, 3, 2))
    for n in ("hy_w1", "hy_b1", "hy_freq1", "hy_w2", "hy_b2", "hy_freq2", "hy_w3"):
        out[n] = np.ascontiguousarray(np.asarray(w[n], f32))
    for (L, tag) in ((t_lat, "l"), (t_ctx, "c")):
        tb = hy_tables(L)
        for k_, v in tb.items():
            out[f"hy_{k_}_{tag}"] = v
    return out


T_CTX_FULL, T_LAT_FULL = 256, 4096
_NC_CACHE = {}


def _get_nc():
    if "nc" not in _NC_CACHE:
        cfg = dict(t_ctx=T_CTX_FULL, t_lat=T_LAT_FULL, depth=DEPTH, half=17)
        _NC_CACHE["nc"] = build(cfg)
    return _NC_CACHE["nc"]


def kernel(**inputs):
    f32 = np.float32
    w = {k_: np.asarray(v) for k_, v in inputs.items()}
    B = w["x"].shape[0]
    shared = dict(
        norm_g=w["norm_g"], w_mod=w["w_mod"], b_mod=w["b_mod"], w_in=w["w_in"], w_pa=w["w_pa"], w_pb=w["w_pb"],
        w_pc=w["w_pc"], w_out=w["w_out"], final_g=w["final_g"], ident=np.eye(128, dtype=f32),
        q_norm_g=w["q_norm_g"], k_norm_g=w["k_norm_g"], rope=rope_table(T_LAT_FULL))
    shared.update(hyena_inputs(w, T_CTX_FULL, T_LAT_FULL))
    shared.update(deltanet_inputs(w))
    shared = {k_: np.ascontiguousarray(v, dtype=f32) for k_, v in shared.items()}
    in_maps = []
    for b in range(B):
        csT = np.stack([w["c"][b].reshape(NKC, 128).T, w["c_ctx"].reshape(NKC, 128).T], axis=-1)
        m = dict(shared)
        m["x"] = np.ascontiguousarray(w["x"][b], dtype=f32)
        m["ctx"] = np.ascontiguousarray(w["ctx"][b], dtype=f32)
        m["csT"] = np.ascontiguousarray(csT, dtype=f32)
        in_maps.append(m)
    nc = _get_nc()
    res = run_bass_kernel_spmd(nc, in_maps, core_ids=list(range(B)))
    return np.stack([np.asarray(r["out"], dtype=f32) for r in res.results], axis=0)
```

```python
from contextlib import ExitStack
import math
import numpy as np
import ml_dtypes
import concourse.bass as bass
import concourse.mybir as mybir
from concourse.alu_op_type import AluOpType as ALU
from concourse.bass_utils import run_bass_kernel_spmd

F32 = mybir.dt.float32
BF16 = mybir.dt.bfloat16
AF = mybir.ActivationFunctionType
AX = mybir.AxisListType
ENGS = ("pe", "dve", "act", "pool", "sp")
NDS = 6
ARENA_WORDS = 44 * 1024


class Res:
    __slots__ = ("w", "rd")

    def __init__(self):
        self.w = None
        self.rd = []


class Tile:
    def __init__(self, ap, nres=1):
        self.ap = ap
        self.res = [Res() for _ in range(nres)]
        self.r = self.res[0]

    def __getitem__(self, k):
        return self.ap[k]


class Prog:
    def __init__(self):
        self.nc = bass.Bass("TRN2", target_bir_lowering=False)
        self.st = ExitStack()
        nc = self.nc
        self.eng = {"pe": nc.tensor, "dve": nc.vector, "act": nc.scalar, "pool": nc.gpsimd, "sp": nc.sync}
        self.sem = {e: self.st.enter_context(nc.semaphore("s_" + e)) for e in ENGS}
        self.cnt = {e: 0 for e in ENGS}
        self.seen = {e: {} for e in ENGS}
        self.dsem, self.dcnt, self.dnext = {}, {}, {}
        for qn in ("sp", "pool", "act"):
            self.dsem[qn] = [self.st.enter_context(nc.semaphore(f"d_{qn}{i}")) for i in range(NDS)]
            self.dcnt[qn] = [0] * NDS
            self.dnext[qn] = 0
        self.stacks = [ExitStack()]
        self.used = [0]
        self.nname = 0
        self.pst = None
        self.banks = [Tile(None) for _ in range(8)]
        self._fresh_psum()
        self.ninst = 0
        self.nwait = 0

    def _fresh_psum(self):
        if self.pst is not None:
            self.pst.close()
        self.pst = ExitStack()
        self.nname += 1
        for i in range(8):
            h = self.pst.enter_context(self.nc.psum_tensor(f"ps{self.nname}_{i}", [128, 512], F32))
            self.banks[i].ap = h[:, :]

    def dram(self, name, shape, dt, kind):
        return self.nc.dram_tensor(name, list(shape), dt, kind=kind).ap()

    def mark(self):
        self.stacks.append(ExitStack())
        self.used.append(self.used[-1])
        return len(self.stacks) - 1

    def release(self, m):
        while len(self.stacks) > m:
            self.stacks.pop().close()
            self.used.pop()
        self.stacks.append(ExitStack())
        self.used.append(self.used[-1])
        self.barrier()

    def alloc(self, free_shape, dt, nres=1):
        n = int(np.prod(free_shape))
        nbytes = (n * (2 if dt == BF16 else 4) + 31) // 32 * 32
        self.used[-1] += nbytes
        assert self.used[-1] <= ARENA_WORDS * 4, f"SBUF overflow {self.used[-1]}"
        self.nname += 1
        h = self.stacks[-1].enter_context(self.nc.sbuf_tensor(f"t{self.nname}", [128, n], dt))
        ap = h[:, :]
        if len(free_shape) == 2:
            ap = ap.rearrange("p (a b) -> p a b", a=free_shape[0])
        elif len(free_shape) == 3:
            ap = ap.rearrange("p (a b c) -> p a b c", a=free_shape[0], b=free_shape[1])
        elif len(free_shape) == 4:
            ap = ap.rearrange("p (a b c d) -> p a b c d", a=free_shape[0], b=free_shape[1], c=free_shape[2])
        return Tile(ap, nres)

    def _wait(self, e, tok):
        if tok is None:
            return
        sem, val = tok
        k = id(sem)
        if self.seen[e].get(k, 0) >= val:
            return
        self.seen[e][k] = val
        self.eng[e].wait_ge(sem, val)
        self.nwait += 1

    @staticmethod
    def _rl(xs):
        return [x.r if isinstance(x, Tile) else x for x in xs]

    def _deps(self, e, r, w):
        for x in r:
            self._wait(e, x.w)
        for x in w:
            self._wait(e, x.w)
            for t in x.rd:
                self._wait(e, t)

    @staticmethod
    def _commit(tok, r, w):
        for x in r:
            x.rd.append(tok)
            if len(x.rd) > 32:
                x.rd = x.rd[-32:]
        for x in w:
            x.w = tok
            x.rd = []

    def op(self, e, fn, r=(), w=()):
        r, w = self._rl(r), self._rl(w)
        self._deps(e, r, w)
        self.cnt[e] += 1
        tok = (self.sem[e], self.cnt[e])
        fn(self.eng[e]).then_inc(self.sem[e], 1)
        self._commit(tok, r, w)
        self.ninst += 1
        return tok

    def dma(self, qn, out, in_, r=(), w=(), **kw):
        r, w = self._rl(r), self._rl(w)
        self._deps(qn, r, w)
        i = self.dnext[qn]
        self.dnext[qn] = (i + 1) % NDS
        sem = self.dsem[qn][i]
        if self.dcnt[qn][i] > 0:
            self._wait(qn, (sem, 16 * self.dcnt[qn][i]))
        self.dcnt[qn][i] += 1
        tok = (sem, 16 * self.dcnt[qn][i])
        self.eng[qn].dma_start(out=out, in_=in_, **kw).then_inc(sem, 16)
        self._commit(tok, r, w)
        self.ninst += 1
        return tok

    def barrier(self):
        toks = [(self.sem[e], self.cnt[e]) for e in ENGS if self.cnt[e] > 0]
        for qn in self.dsem:
            for i, sem in enumerate(self.dsem[qn]):
                if self.dcnt[qn][i] > 0:
                    toks.append((sem, 16 * self.dcnt[qn][i]))
        for e in ENGS:
            for t in toks:
                if t[0] is not self.sem[e]:
                    self._wait(e, t)
        self._fresh_psum()

    def finish(self):
        self.barrier()
        while self.stacks:
            self.stacks.pop().close()
        self.pst.close()
        self.st.close()
        return self.nc


D = 2048
NKC = 16
DEPTH = 4
IN_W = 15904
FM0, FM1 = 1536, 8704
NFM = FM1 - FM0
NTM = FM0 + (IN_W - FM1)
TM_Z, TM_A, TM_B, TM_MG = 1536, 2560, 2576, 2592
FM_AG, FM_HV, FM_HX1, FM_HX2, FM_HG, FM_DQ, FM_DK, FM_DV = 0, 1024, 2048, 3072, 4096, 5120, 5632, 6144
EPS = 1e-6


class K:
    def __init__(self, P):
        self.P = P

    def mm(self, out, lhsT, rhs, start, stop, r, w):
        self.P.op("pe", lambda e: e.matmul(out=out, lhsT=lhsT, rhs=rhs, start=start, stop=stop), r, w)

    def tr(self, out, in_, ident, r, w):
        self.P.op("pe", lambda e: e.transpose(out=out, in_=in_, identity=ident), r, w)

    def tt(self, eng, out, in0, in1, op, r, w):
        self.P.op(eng, lambda e: e.tensor_tensor(out=out, in0=in0, in1=in1, op=op), r, w)

    def ts(self, eng, out, in0, s1, s2, op0, op1, r, w):
        if op1 is None:
            self.P.op(eng, lambda e: e.tensor_scalar(out=out, in0=in0, scalar1=s1, scalar2=None, op0=op0), r, w)
        else:
            self.P.op(eng, lambda e: e.tensor_scalar(out=out, in0=in0, scalar1=s1, scalar2=s2, op0=op0, op1=op1), r, w)

    def stt(self, out, in0, scalar, in1, op0, op1, r, w):
        self.P.op("dve", lambda e: e.scalar_tensor_tensor(out=out, in0=in0, scalar=scalar, in1=in1, op0=op0, op1=op1), r, w)

    def act(self, out, in_, func, r, w, bias=None, scale=None):
        kw = {}
        if bias is not None:
            kw["bias"] = bias
        if scale is not None:
            kw["scale"] = scale
        self.P.op("act", lambda e: e.activation(out=out, in_=in_, func=func, **kw), r, w)

    def copy(self, eng, out, in_, r, w):
        if eng == "act":
            self.P.op("act", lambda e: e.activation(out=out, in_=in_, func=AF.Copy), r, w)
        else:
            self.P.op(eng, lambda e: e.tensor_copy(out=out, in_=in_), r, w)

    def recip(self, out, in_, r, w):
        self.P.op("dve", lambda e: e.reciprocal(out=out, in_=in_), r, w)

    def ssq(self, junk, in_, acc, r, w):
        self.P.op("act", lambda e: e.activation(out=junk, in_=in_, func=AF.Square, accum_out=acc), r, w)

    def memset(self, eng, ap, val, w):
        self.P.op(eng, lambda e: e.memset(ap, val), (), w)


def bank_bf(P, i):
    return P.banks[i].ap.bitcast(BF16)


def tok_groups(n, g=512):
    out, t0 = [], 0
    while t0 < n:
        out.append((t0, min(g, n - t0)))
        t0 += g
    return out


def build(cfg):
    T_CTX, T_LAT, depth = cfg["t_ctx"], cfg["t_lat"], cfg["depth"]
    dbg = cfg.get("dbg", ())
    T = T_CTX + T_LAT
    NT = T // 128
    NCT = T_CTX // 128
    HALF = cfg.get("half", 17)
    assert NT % HALF == 0
    P = Prog()
    k = K(P)
    def inp(n, s, dt=F32):
        if n in cfg.get("dummy", ()):
            s = [1] * len(s)
        return P.dram(n, s, dt, "ExternalInput")
    WD = cfg.get("wdepth", DEPTH)
    x_in = inp("x", [T_LAT, D])
    ctx_in = inp("ctx", [T_CTX, D])
    cs_in = inp("csT", [128, NKC, 2])
    norm_g = inp("norm_g", [WD, D])
    w_mod = inp("w_mod", [WD, D, 3 * D])
    b_mod = inp("b_mod", [WD, 3 * D])
    w_in = inp("w_in", [WD, D, IN_W])
    w_pa = inp("w_pa", [WD, 1024, D])
    w_pb = inp("w_pb", [WD, 1024, D])
    w_pc = inp("w_pc", [WD, 1024, D])
    w_out = inp("w_out", [WD, D, D])
    final_g = inp("final_g", [D])
    ident_in = inp("ident", [128, 128])
    out = P.dram("out", [T_LAT, D], F32, "ExternalOutput")

    def scratch(n, s, dt=F32):
        kind = "ExternalOutput" if n in dbg else ("ExternalInput" if n in cfg.get("ext_in", ()) else "Internal")
        return P.dram(n, s, dt, kind)

    xs = scratch("xs", [T, D])
    pTM = scratch("pTM", [T, NTM])
    pFM = scratch("pFM", [NFM, T])
    modS = scratch("modS", [2, 3 * D])
    yT = scratch("yT", [3072, T], BF16)
    mT = scratch("mT", [D, T], BF16)
    R_xs, R_pTM, R_pFM, R_modS, R_yT, R_mT = Res(), Res(), Res(), Res(), Res(), Res()

    identf = P.alloc([128], F32)
    identb = P.alloc([128], BF16)
    cs = P.alloc([NKC, 2], F32)
    P.dma("sp", identf[:], ident_in[:, :], w=[identf])
    P.dma("pool", identb[:], ident_in[:, :], w=[identb])
    P.dma("sp", cs[:], cs_in[:, :, :], w=[cs])
    k.act(cs[:], cs[:], AF.Silu, [cs], [cs])
    base_mark = P.mark()
    rr = {"bank": 0, "ev": 0}

    def next_bank(lo=2, n=4):
        b = P.banks[lo + rr["bank"] % n]
        rr["bank"] += 1
        return b

    def evac(out_ap, in_ap, r, w):
        eng = "act" if rr["ev"] % 2 == 0 else "dve"
        rr["ev"] += 1
        k.copy(eng, out_ap, in_ap, r, w)

    def xrows(l, t):
        if l == 0:
            if t < NCT:
                return ctx_in[t * 128:(t + 1) * 128, :]
            return x_in[(t - NCT) * 128:(t - NCT + 1) * 128, :]
        return xs[t * 128:(t + 1) * 128, :]

    def phase_mod(l):
        m0 = P.mark()
        bm = P.alloc([3 * D], F32)
        mo = P.alloc([3 * D], F32)
        wst = [P.alloc([NKC, 512], F32) for _ in range(2)]
        P.dma("sp", bm[0:2, :], b_mod[l].partition_broadcast(2), w=[bm])
        for cb in range(12):
            ws = wst[cb % 2]
            P.dma("sp", ws[:], w_mod[l][:, cb * 512:(cb + 1) * 512].rearrange("(kc p) n -> p kc n", p=128), w=[ws])
            bank = P.banks[6 + cb % 2]
            for kc in range(NKC):
                k.mm(bank[0:2, :], cs[:, kc, :], ws[:, kc, :], kc == 0, kc == NKC - 1, [cs, ws], [bank])
            k.tt("dve", mo[0:2, cb * 512:(cb + 1) * 512], bank[0:2, :], bm[0:2, cb * 512:(cb + 1) * 512], ALU.add,
                 [bank, bm], [mo])
        P.dma("sp", modS[:, :], mo[0:2, :], r=[mo], w=[R_modS])
        P.release(m0)

    def phase_inproj(l):
        m0 = P.mark()
        gbc = P.alloc([D], F32)
        A = [P.alloc([D], F32) for _ in range(2)]
        B = [P.alloc([D], F32) for _ in range(2)]
        P.dma("sp", gbc[:], norm_g[l].partition_broadcast(128), w=[gbc])
        for wch in range(2):
            P.dma("sp", A[wch][:], modS[wch, D:2 * D].partition_broadcast(128), r=[R_modS], w=[A[wch]])
            P.dma("sp", B[wch][:], modS[wch, 0:D].partition_broadcast(128), r=[R_modS], w=[B[wch]])
            k.stt(A[wch][:], A[wch][:], 1.0, gbc[:], ALU.add, ALU.mult, [A[wch], gbc], [A[wch]])
        HT = HALF * 128
        hT = P.alloc([NKC, HT], BF16)
        xt1 = P.alloc([D], F32)
        xt2 = [xt1, xt1]
        tmp = P.alloc([D], F32)
        hb = P.alloc([D], BF16)
        ss = P.alloc([2], F32)
        wb2 = [P.alloc([NKC, 512], BF16) for _ in range(2)]
        ot4 = [P.alloc([512], F32) for _ in range(4)]
        blocks = []
        for c0 in range(0, IN_W, 512):
            n = min(512, IN_W - c0)
            if FM0 <= c0 < FM1:
                blocks.append(("fm", c0, n, c0 - FM0))
            elif c0 < FM0:
                blocks.append(("tm", c0, n, c0))
            else:
                blocks.append(("tm", c0, n, FM0 + c0 - FM1))
        oi = 0
        for half in range(NT // HALF):
            for tl in range(HALF):
                t = half * HALF + tl
                xt = xt2[tl % 2]
                P.dma("sp", xt[:], xrows(l, t), r=[R_xs], w=[xt])
                k.ssq(tmp[:], xt[:], ss[:, 0:1], [xt], [tmp, ss])
                k.ts("dve", ss[:, 0:1], ss[:, 0:1], 1.0 / D, EPS, ALU.mult, ALU.add, [ss], [ss])
                k.act(ss[:, 0:1], ss[:, 0:1], AF.Sqrt, [ss], [ss])
                k.recip(ss[:, 1:2], ss[:, 0:1], [ss], [ss])
                wch = 1 if t < NCT else 0
                k.stt(tmp[:], xt[:], ss[:, 1:2], A[wch][:], ALU.mult, ALU.mult, [xt, ss, A[wch]], [tmp])
                k.tt("dve", hb[:], tmp[:], B[wch][:], ALU.add, [tmp, B[wch]], [hb])
                for g in range(2):
                    bank = P.banks[g]
                    bv = bank_bf(P, g).rearrange("p (a b) -> p a b", a=8)
                    for j in range(8):
                        kc = g * 8 + j
                        k.tr(bv[:, j, :], hb[:, kc * 128:(kc + 1) * 128], identb[:], [hb, identb], [bank])
                    evac(hT[:, g * 8:(g + 1) * 8, tl * 128:(tl + 1) * 128], bv[:, :, :], [bank], [hT])
            for bi, (kind, c0, n, dst) in enumerate(blocks):
                wb = wb2[bi % 2]
                P.dma("pool", wb[:, :, 0:n], w_in[l][:, c0:c0 + n].rearrange("(kc p) n -> p kc n", p=128), w=[wb])
                if kind == "tm":
                    for tl in range(HALF):
                        t = half * HALF + tl
                        bank = next_bank()
                        for kc in range(NKC):
                            k.mm(bank[:, 0:n], hT[:, kc, tl * 128:(tl + 1) * 128], wb[:, kc, 0:n], kc == 0, kc == NKC - 1,
                                 [hT, wb], [bank])
                        ot = ot4[oi % 4]
                        oi += 1
                        evac(ot[:, 0:n], bank[:, 0:n], [bank], [ot])
                        P.dma("sp", pTM[t * 128:(t + 1) * 128, dst:dst + n], ot[:, 0:n], r=[ot], w=[R_pTM])
                else:
                    for ch in range(n // 128):
                        for (t0, tn) in tok_groups(HT):
                            bank = next_bank()
                            for kc in range(NKC):
                                k.mm(bank[:, 0:tn], wb[:, kc, ch * 128:(ch + 1) * 128], hT[:, kc, t0:t0 + tn], kc == 0,
                                     kc == NKC - 1, [hT, wb], [bank])
                            ot = ot4[oi % 4]
                            oi += 1
                            evac(ot[:, 0:tn], bank[:, 0:tn], [bank], [ot])
                            P.dma("sp", pFM[dst + ch * 128:dst + (ch + 1) * 128, half * HT + t0:half * HT + t0 + tn],
                                  ot[:, 0:tn], r=[ot], w=[R_pFM])
        P.release(m0)

    def phase_merge(l, last):
        m0 = P.mark()
        t_first = NCT if last else 0
        wp = [P.alloc([8, D], BF16) for _ in range(3)]
        for br, wsrc in enumerate((w_pa, w_pb, w_pc)):
            for hh in range(2):
                P.dma("pool", wp[br][:, hh * 4:(hh + 1) * 4, :],
                      wsrc[l][hh * 512:(hh + 1) * 512, :].rearrange("(kc p) n -> p kc n", p=128), w=[wp[br]])
        yt2 = [P.alloc([24, 128], BF16) for _ in range(2)]
        lg1 = P.alloc([3 * D], F32)
        lg2 = [lg1, lg1]
        macc = P.alloc([D], F32)
        mtmp = P.alloc([512], F32)
        mb = P.alloc([D], BF16)
        mTt = [P.alloc([NKC, 128], BF16) for _ in range(2)]
        for t in range(t_first, NT):
            yt = yt2[t % 2]
            lg = lg2[t % 2]
            P.dma("sp", yt[:], yT[:, t * 128:(t + 1) * 128].rearrange("(c p) t -> p c t", p=128), r=[R_yT], w=[yt])
            P.dma("sp", lg[:], pTM[t * 128:(t + 1) * 128, TM_MG:TM_MG + 3 * D], r=[R_pTM], w=[lg])
            k.act(lg[:], lg[:], AF.Sigmoid, [lg], [lg])
            for cb in range(4):
                cs_ = slice(cb * 512, (cb + 1) * 512)
                for br in range(3):
                    bank = next_bank()
                    for kc in range(8):
                        k.mm(bank[:, :], yt[:, br * 8 + kc, :], wp[br][:, kc, cs_], kc == 0, kc == 7, [yt, wp[br]], [bank])
                    gs = slice(br * D + cb * 512, br * D + (cb + 1) * 512)
                    if br == 0:
                        k.tt("dve", macc[:, cs_], bank[:, :], lg[:, gs], ALU.mult, [bank, lg], [macc])
                    else:
                        k.tt("dve", mtmp[:], bank[:, :], lg[:, gs], ALU.mult, [bank, lg], [mtmp])
                        k.tt("pool", macc[:, cs_], macc[:, cs_], mtmp[:], ALU.add, [macc, mtmp], [macc])
            k.copy("act", mb[:], macc[:], [macc], [mb])
            mt = mTt[t % 2]
            for g in range(2):
                bank = P.banks[g]
                bv = bank_bf(P, g).rearrange("p (a b) -> p a b", a=8)
                for j in range(8):
                    kc = g * 8 + j
                    k.tr(bv[:, j, :], mb[:, kc * 128:(kc + 1) * 128], identb[:], [mb, identb], [bank])
                evac(mt[:, g * 8:(g + 1) * 8, :], bv[:, :, :], [bank], [mt])
            P.dma("sp", mT[:, t * 128:(t + 1) * 128].rearrange("(c p) t -> p c t", p=128), mt[:], r=[mt], w=[R_mT])
        P.release(m0)
        P.barrier()
        wo = P.alloc([NKC, D], BF16)
        for hh in range(4):
            P.dma("pool", wo[:, hh * 4:(hh + 1) * 4, :],
                  w_out[l][hh * 512:(hh + 1) * 512, :].rearrange("(kc p) n -> p kc n", p=128), w=[wo])
        gate = [P.alloc([D], F32) for _ in range(2)]
        for wch in range(2):
            P.dma("sp", gate[wch][:], modS[wch, 2 * D:3 * D].partition_broadcast(128), r=[R_modS], w=[gate[wch]])
        if last:
            fg = P.alloc([D], F32)
            P.dma("sp", fg[:], final_g.partition_broadcast(128), w=[fg])
        mt2 = [P.alloc([NKC, 128], BF16) for _ in range(2)]
        xt2 = [P.alloc([D], F32) for _ in range(2)]
        xn2 = [P.alloc([D], F32) for _ in range(2)]
        tmp = P.alloc([D], F32)
        ss = P.alloc([2], F32)
        for t in range(t_first, NT):
            mt, xt, xn = mt2[t % 2], xt2[t % 2], xn2[t % 2]
            wch = 1 if t < NCT else 0
            P.dma("sp", mt[:], mT[:, t * 128:(t + 1) * 128].rearrange("(c p) t -> p c t", p=128), r=[R_mT], w=[mt])
            P.dma("sp", xt[:], xrows(l, t), r=[R_xs], w=[xt])
            for cb in range(4):
                cs_ = slice(cb * 512, (cb + 1) * 512)
                bank = next_bank()
                for kc in range(NKC):
                    k.mm(bank[:, :], mt[:, kc, :], wo[:, kc, cs_], kc == 0, kc == NKC - 1, [mt, wo], [bank])
                k.tt("dve", xn[:, cs_], bank[:, :], gate[wch][:, cs_], ALU.mult, [bank, gate[wch]], [xn])
                k.tt("pool", xn[:, cs_], xn[:, cs_], xt[:, cs_], ALU.add, [xn, xt], [xn])
            if not last:
                P.dma("sp", xs[t * 128:(t + 1) * 128, :], xn[:], r=[xn], w=[R_xs])
            else:
                k.ssq(tmp[:], xn[:], ss[:, 0:1], [xn], [tmp, ss])
                k.ts("dve", ss[:, 0:1], ss[:, 0:1], 1.0 / D, EPS, ALU.mult, ALU.add, [ss], [ss])
                k.act(ss[:, 0:1], ss[:, 0:1], AF.Sqrt, [ss], [ss])
                k.recip(ss[:, 1:2], ss[:, 0:1], [ss], [ss])
                k.stt(tmp[:], xn[:], ss[:, 1:2], fg[:], ALU.mult, ALU.mult, [xn, ss, fg], [tmp])
                P.dma("sp", out[(t - NCT) * 128:(t - NCT + 1) * 128, :], tmp[:], r=[tmp])
        P.release(m0)

    env = dict(P=P, k=k, cfg=cfg, T=T, NT=NT, NCT=NCT, T_CTX=T_CTX, T_LAT=T_LAT, pTM=pTM, pFM=pFM, yT=yT,
               R_pTM=R_pTM, R_pFM=R_pFM, R_yT=R_yT, identf=identf, identb=identb, next_bank=next_bank, evac=evac,
               inp=inp)
    mixers = cfg.get("mixers", ("att", "hy", "dn"))
    mix_fns = {}
    if "att" in mixers:
        mix_fns["att"] = make_attention(env)
    if "hy" in mixers:
        mix_fns["hy"] = make_hyena(env)
    if "dn" in mixers:
        mix_fns["dn"] = make_deltanet(env)
    phases = cfg.get("phases", ("mod", "inproj", "mix", "merge"))
    for l in range(depth):
        last = (l == DEPTH - 1) if cfg.get("real_last", True) else (l == depth - 1)
        if "mod" in phases:
            phase_mod(l)
            P.barrier()
        if "inproj" in phases:
            phase_inproj(l)
            P.barrier()
        if "mix" in phases:
            for name in mixers:
                mix_fns[name](l, last)
                P.barrier()
        if "merge" in phases:
            phase_merge(l, last)
            P.barrier()
    print("instructions:", P.ninst, "waits:", P.nwait)
    return P.finish()


def make_attention(env):
    P, k, cfg = env["P"], env["k"], env["cfg"]
    T, NT, NCT, T_CTX, T_LAT = env["T"], env["NT"], env["NCT"], env["T_CTX"], env["T_LAT"]
    pTM, pFM, yT = env["pTM"], env["pFM"], env["yT"]
    R_pTM, R_pFM, R_yT = env["R_pTM"], env["R_pFM"], env["R_yT"]
    identb, evac = env["identb"], env["evac"]
    WD = cfg.get("wdepth", DEPTH)
    q_norm_g = env["inp"]("q_norm_g", [WD, 128])
    k_norm_g = env["inp"]("k_norm_g", [WD, 128])
    rope = env["inp"]("rope", [T_LAT, 2, 128])

    def run(l, last):
        m0 = P.mark()
        qT = P.alloc([8, T], BF16)
        kT = P.alloc([2, T], BF16)
        Vb = P.alloc([NT, 256], BF16)
        gqk = P.alloc([10, 128], F32)
        onesb = P.alloc([128], BF16)
        k.memset("dve", onesb[:], 1.0, [onesb])
        for h in range(10):
            src = q_norm_g[l] if h < 8 else k_norm_g[l]
            P.dma("sp", gqk[:, h, :], src.partition_broadcast(128), w=[gqk])
        k.ts("dve", gqk[:, 0:8, :], gqk[:, 0:8, :], 128.0 ** -0.5, None, ALU.mult, None, [gqk], [gqk])
        m1 = P.mark()
        qk2 = [P.alloc([10, 128], F32) for _ in range(2)]
        cs2 = [P.alloc([2, 128], F32) for _ in range(2)]
        sq = P.alloc([10, 128], F32)
        xn = P.alloc([10, 128], F32)
        t1 = P.alloc([10, 128], F32)
        t2 = P.alloc([10, 128], F32)
        ob = P.alloc([10, 128], BF16)
        st = P.alloc([2, 10], F32)
        for t in range(NT):
            qk = qk2[t % 2]
            rows = slice(t * 128, (t + 1) * 128)
            P.dma("sp", qk[:], pTM[rows, 0:1280].rearrange("p (h d) -> p h d", h=10), r=[R_pTM], w=[qk])
            P.dma("pool", Vb[:, t, :], pTM[rows, 1280:1536], r=[R_pTM], w=[Vb])
            k.tt("pool", sq[:], qk[:], qk[:], ALU.mult, [qk], [sq])
            P.op("dve", lambda e, o=st[:, 0, :], i=sq[:]: e.tensor_reduce(out=o, in_=i, axis=AX.X, op=ALU.add), [sq], [st])
            k.ts("dve", st[:, 0, :], st[:, 0, :], 1.0 / 128, EPS, ALU.mult, ALU.add, [st], [st])
            k.act(st[:, 0, :], st[:, 0, :], AF.Sqrt, [st], [st])
            k.recip(st[:, 1, :], st[:, 0, :], [st], [st])
            k.tt("dve", xn[:], qk[:], st[:, 1, :].unsqueeze(2).to_broadcast([128, 10, 128]), ALU.mult, [qk, st], [xn])
            if t < NCT:
                k.tt("dve", ob[:], xn[:], gqk[:], ALU.mult, [xn, gqk], [ob])
            else:
                cs_ = cs2[t % 2]
                P.dma("sp", cs_[:], rope[(t - NCT) * 128:(t - NCT + 1) * 128, :, :], w=[cs_])
                k.tt("pool", xn[:], xn[:], gqk[:], ALU.mult, [xn, gqk], [xn])
                k.tt("dve", t1[:], xn[:], cs_[:, 0:1, :].to_broadcast([128, 10, 128]), ALU.mult, [xn, cs_], [t1])
                xn4 = xn[:].rearrange("p h (a j) -> p h a j", a=2)
                t24 = t2[:].rearrange("p h (a j) -> p h a j", a=2)
                s4 = cs_[:, 1:2, :].rearrange("p o (a j) -> p o a j", a=2)
                k.tt("pool", t24[:, :, :, 0:32], xn4[:, :, :, 32:64], s4[:, :, :, 0:32].to_broadcast([128, 10, 2, 32]),
                     ALU.mult, [xn, cs_], [t2])
                k.tt("pool", t24[:, :, :, 32:64], xn4[:, :, :, 0:32], s4[:, :, :, 32:64].to_broadcast([128, 10, 2, 32]),
                     ALU.mult, [xn, cs_], [t2])
                k.tt("dve", ob[:], t1[:], t2[:], ALU.add, [t1, t2], [ob])
            for g, (h0, nh) in enumerate(((0, 8), (8, 2))):
                bank = P.banks[g]
                bv = bank_bf(P, g).rearrange("p (a b) -> p a b", a=8)
                for j in range(nh):
                    k.tr(bv[:, j, :], ob[:, h0 + j, :], identb[:], [ob, identb], [bank])
                if g == 0:
                    evac(qT[:, :, rows], bv[:, :, :], [bank], [qT])
                else:
                    evac(kT[:, :, rows], bv[:, 0:2, :], [bank], [kT])
        P.release(m1)
        et3 = [P.alloc([512], BF16) for _ in range(3)]
        g2 = [P.alloc([512], F32) for _ in range(2)]
        rec2 = [P.alloc([512], F32) for _ in range(2)]
        o2 = [P.alloc([512], F32) for _ in range(2)]
        ob2 = [P.alloc([512], BF16) for _ in range(2)]
        groups = [(0, T_CTX, 0, NCT)] if not last else []
        groups += [(T_CTX + t0, tn, 0, NT) for (t0, tn) in tok_groups(T_LAT)]
        gi = 0
        ei = 0
        for h in range(8):
            kvh = h // 4
            for (q0, N, kt0, kt1) in groups:
                bo = P.banks[4 + 2 * (gi % 2)]
                bd = P.banks[5 + 2 * (gi % 2)]
                for kt in range(kt0, kt1):
                    bs = P.banks[2 + ei % 2]
                    et = et3[ei % 3]
                    ei += 1
                    k.mm(bs[:, 0:N], kT[:, kvh, kt * 128:(kt + 1) * 128], qT[:, h, q0:q0 + N], True, True, [kT, qT], [bs])
                    k.act(et[:, 0:N], bs[:, 0:N], AF.Exp, [bs], [et])
                    k.mm(bo[:, 0:N], Vb[:, kt, kvh * 128:(kvh + 1) * 128], et[:, 0:N], kt == kt0, kt == kt1 - 1, [Vb, et], [bo])
                    k.mm(bd[:, 0:N], onesb[:], et[:, 0:N], kt == kt0, kt == kt1 - 1, [onesb, et], [bd])
                gt, rec, o, obf = g2[gi % 2], rec2[gi % 2], o2[gi % 2], ob2[gi % 2]
                P.dma("sp", gt[:, 0:N], pFM[FM_AG + h * 128:FM_AG + (h + 1) * 128, q0:q0 + N], r=[R_pFM], w=[gt])
                k.act(gt[:, 0:N], gt[:, 0:N], AF.Silu, [gt], [gt])
                k.recip(rec[:, 0:N], bd[:, 0:N], [bd], [rec])
                k.tt("dve", o[:, 0:N], bo[:, 0:N], rec[:, 0:N], ALU.mult, [bo, rec], [o])
                k.tt("pool", obf[:, 0:N], o[:, 0:N], gt[:, 0:N], ALU.mult, [o, gt], [obf])
                P.dma("sp", yT[h * 128:(h + 1) * 128, q0:q0 + N], obf[:, 0:N], r=[obf], w=[R_yT])
                gi += 1
        P.release(m0)

    return run


def rope_table(t_lat):
    pos = np.arange(t_lat)
    inv = 10000.0 ** (-np.arange(0, 64, 2, dtype=np.float32) / 64).astype(np.float32)
    tab = np.zeros((t_lat, 2, 128), np.float32)
    for a, p in enumerate(((pos // 64).astype(np.float32), (pos % 64).astype(np.float32))):
        ang = (p[:, None] * inv[None, :]).astype(np.float32)
        c, s = np.cos(ang), np.sin(ang)
        tab[:, 0, a * 64:a * 64 + 32] = c
        tab[:, 0, a * 64 + 32:a * 64 + 64] = c
        tab[:, 1, a * 64:a * 64 + 32] = -s
        tab[:, 1, a * 64 + 32:a * 64 + 64] = s
    return tab


TWO_PI = 2.0 * math.pi


def hy_tables(L):
    N = 2 * L
    N1 = N // 128
    H1 = N1 // 2
    f = np.float64
    s1 = np.arange(H1, dtype=f)[:, None]
    f1 = np.arange(N1, dtype=f)[None, :]
    ang = TWO_PI * s1 * f1 / N1
    DA = np.zeros((N1, 4 * N1), f)
    for c in range(2):
        DA[c * H1:(c + 1) * H1, c * 2 * N1:c * 2 * N1 + N1] = np.cos(ang)
        DA[c * H1:(c + 1) * H1, c * 2 * N1 + N1:(c + 1) * 2 * N1] = -np.sin(ang)
    s2 = np.arange(128, dtype=f)[:, None]
    a1 = TWO_PI * s2 * np.arange(N1, dtype=f)[None, :] / N
    TW1 = np.stack([np.cos(a1), np.sin(a1), -np.sin(a1)], axis=1)
    a2 = TWO_PI * s2 * np.arange(128, dtype=f)[None, :] / 128
    C, S = np.cos(a2), np.sin(a2)
    DB = np.stack([C, S, -S], axis=1)
    IC = np.stack([np.concatenate([C, S], 1), np.concatenate([-S, C], 1)], axis=1)
    f1c = np.arange(N1, dtype=f)[:, None]
    a3 = TWO_PI * f1c * np.arange(128, dtype=f)[None, :] / N
    t2 = np.stack([np.cos(a3), -np.sin(a3), np.sin(a3)], axis=1)
    TW2 = np.concatenate([t2, t2], axis=0)
    a4 = TWO_PI * np.arange(N1, dtype=f)[:, None] * np.arange(H1, dtype=f)[None, :] / N1
    DD = np.zeros((2 * N1, 2, 2 * H1), f)
    for c in range(2):
        DD[c * N1:(c + 1) * N1, 0, c * H1:(c + 1) * H1] = np.cos(a4) / N
        DD[c * N1:(c + 1) * N1, 1, c * H1:(c + 1) * H1] = -np.sin(a4) / N
    pos = np.arange(L, dtype=np.float32)
    t = pos / max(L - 1, 1)
    bands = np.linspace(1e-4, 15, 16, dtype=np.float32)
    angz = (np.float32(TWO_PI / L) * pos[:, None] * bands).astype(np.float32)
    zT = np.concatenate([t[:, None], np.cos(angz), np.sin(angz)], axis=-1).T
    out = dict(DA=DA, TW1=TW1, DB=DB, IC=IC, TW2=TW2, DD=DD, zT=zT, trow=t)
    return {k_: np.ascontiguousarray(v, dtype=np.float32) for k_, v in out.items()}


def hy_delta_table():
    d = np.abs(np.linspace(math.log(1e-2) / 1.5, math.log(1e-2) / 0.3, 1024, dtype=np.float32))
    return np.ascontiguousarray(d.reshape(8, 128).T)


def make_hyena(env):
    P, k, cfg = env["P"], env["k"], env["cfg"]
    T, NT, NCT, T_CTX, T_LAT = env["T"], env["NT"], env["NCT"], env["T_CTX"], env["T_LAT"]
    pFM, yT = env["pFM"], env["yT"]
    R_pFM, R_yT = env["R_pFM"], env["R_yT"]
    inp = env["inp"]
    WD = cfg.get("wdepth", DEPTH)
    NG = cfg.get("hy_groups", 8)
    hy_cw = inp("hy_cw", [WD, 128, 3, 24])
    hy_cb = inp("hy_cb", [WD, 128, 24])
    hy_d = inp("hy_dp", [WD, 2, 2, 512])
    hy_w1 = inp("hy_w1", [WD, 33, 64])
    hy_b1 = inp("hy_b1", [WD, 64])
    hy_f1 = inp("hy_freq1", [WD, 64])
    hy_w2 = inp("hy_w2", [WD, 64, 64])
    hy_b2 = inp("hy_b2", [WD, 64])
    hy_f2 = inp("hy_freq2", [WD, 64])
    hy_w3 = inp("hy_w3", [WD, 64, 4096])
    hy_delta = inp("hy_delta", [128, 8])
    seqs = [(T_LAT, T_CTX, "l"), (T_CTX, 0, "c")]
    tabs = {}
    for (L, r0, tag) in seqs:
        N1 = 2 * L // 128
        tabs[tag] = dict(DA=inp("hy_DA_" + tag, [N1, 4 * N1]), TW1=inp("hy_TW1_" + tag, [128, 3, N1]),
                         DB=inp("hy_DB_" + tag, [128, 3, 128]), IC=inp("hy_IC_" + tag, [128, 2, 256]),
                         TW2=inp("hy_TW2_" + tag, [2 * N1, 3, 128]), DD=inp("hy_DD_" + tag, [2 * N1, 2, N1]),
                         zT=inp("hy_zT_" + tag, [33, L]), trow=inp("hy_trow_" + tag, [L]))
    hyS = P.dram("hyS", [3, 1024, T], F32, "Internal")
    hyK = {tag: P.dram("hyK_" + tag, [4, 1024, L], F32, "Internal") for (L, r0, tag) in seqs}
    R_hyS, R_hyK = Res(), Res()
    rb = {"i": 0}

    def nb():
        b = P.banks[rb["i"] % 8]
        rb["i"] += 1
        return b

    def cmul(A, O, t1, Tc, Tm, Tp, rA, rT, wO, wt1):
        p_, q_, _, n_ = O.shape
        bc4 = lambda ap: ap.unsqueeze(1).unsqueeze(1).to_broadcast([p_, q_, 2, n_])
        bc3 = lambda ap: ap.unsqueeze(1).to_broadcast([p_, q_, n_])
        k.tt("dve", t1, A, bc4(Tc), ALU.mult, rA + rT, wt1)
        k.tt("dve", O[:, :, 0, :], A[:, :, 1, :], bc3(Tm), ALU.mult, rA + rT, wO)
        k.tt("dve", O[:, :, 1, :], A[:, :, 0, :], bc3(Tp), ALU.mult, rA + rT, wO)
        k.tt("pool", O, O, t1, ALU.add, wO + wt1, wO)

    def run(l, last):
        m0 = P.mark()
        cw = P.alloc([3, 24], F32)
        cb = P.alloc([24], F32)
        P.dma("sp", cw[:], hy_cw[l], w=[cw])
        P.dma("sp", cb[:], hy_cb[l], w=[cb])
        xi2 = [P.alloc([T], F32) for _ in range(2)]
        xo2 = [P.alloc([T], F32) for _ in range(2)]
        segs = [(0, T_CTX), (T_CTX, T)] if not last else [(T_CTX, T)]
        ci = 0
        for j in range(3):
            for g in range(NG):
                xi, xo = xi2[ci % 2], xo2[ci % 2]
                ci += 1
                jj = j * 8 + g
                P.dma("sp", xi[:], pFM[FM_HV + j * 1024 + g * 128:FM_HV + j * 1024 + (g + 1) * 128, :], r=[R_pFM], w=[xi])
                k.ts("dve", xo[:], xi[:], cw[:, 1, jj:jj + 1], cb[:, jj:jj + 1], ALU.mult, ALU.add, [xi, cw, cb], [xo])
                for (a, b) in segs:
                    k.stt(xo[:, a + 1:b], xi[:, a:b - 1], cw[:, 0, jj:jj + 1], xo[:, a + 1:b], ALU.mult, ALU.add, [xi, cw, xo], [xo])
                    k.stt(xo[:, a:b - 1], xi[:, a + 1:b], cw[:, 2, jj:jj + 1], xo[:, a:b - 1], ALU.mult, ALU.add, [xi, cw, xo], [xo])
                P.dma("sp", hyS[j, g * 128:(g + 1) * 128, :], xo[:], r=[xo], w=[R_hyS])
        P.release(m0)
        for (L, r0, tag) in seqs:
            if last and tag == "c":
                continue
            m1 = P.mark()
            tb = tabs[tag]
            zT = P.alloc([L], F32)
            trow = P.alloc([L], F32)
            h1 = P.alloc([L], F32)
            h2 = P.alloc([L], F32)
            w1 = P.alloc([64], F32)
            w2 = P.alloc([64], F32)
            w3 = P.alloc([4096], F32)
            col = P.alloc([8], F32)
            dl = P.alloc([8], F32)
            ndl = P.alloc([8], F32)
            P.dma("sp", zT[0:33, :], tb["zT"][:, :], w=[zT])
            P.dma("sp", trow[:], tb["trow"].partition_broadcast(128), w=[trow])
            P.dma("sp", w1[0:33, :], hy_w1[l], w=[w1])
            P.dma("sp", w2[0:64, :], hy_w2[l], w=[w2])
            P.dma("sp", w3[0:64, :], hy_w3[l], w=[w3])
            for i_, src in enumerate((hy_b1, hy_f1, hy_b2, hy_f2)):
                P.dma("sp", col[0:64, i_:i_ + 1], src[l].rearrange("(p o) -> p o", o=1), w=[col])
            P.dma("sp", dl[:], hy_delta[:, :], w=[dl])
            k.ts("dve", ndl[:], dl[:], -1.0, None, ALU.mult, None, [dl], [ndl])
            k.tt("dve", col[0:64, 4:5], col[0:64, 0:1], col[0:64, 1:2], ALU.mult, [col], [col])
            k.tt("dve", col[0:64, 5:6], col[0:64, 2:3], col[0:64, 3:4], ALU.mult, [col], [col])
            ua = P.alloc([512], F32)
            ub = P.alloc([512], F32)
            for (src, wt, kk_, fr, fb, dst) in ((zT, w1, 33, 1, 4, h1), (h1, w2, 64, 3, 5, h2)):
                for (c0, cn) in tok_groups(L):
                    bank = nb()
                    k.mm(bank[0:64, 0:cn], wt[0:kk_, 0:64], src[0:kk_, c0:c0 + cn], True, True, [wt, src], [bank])
                    k.act(ua[0:64, 0:cn], bank[0:64, 0:cn], AF.Identity, [bank, col], [ua], bias=col[0:64, fb:fb + 1],
                          scale=col[0:64, fr:fr + 1])
                    k.ts("dve", ub[0:64, 0:cn], ua[0:64, 0:cn], 1.0 / TWO_PI, 12582912.0, ALU.mult, ALU.add, [ua], [ub])
                    k.ts("dve", ub[0:64, 0:cn], ub[0:64, 0:cn], -12582912.0, None, ALU.add, None, [ub], [ub])
                    k.stt(ua[0:64, 0:cn], ub[0:64, 0:cn], -TWO_PI, ua[0:64, 0:cn], ALU.mult, ALU.add, [ub, ua], [ua])
                    k.ts("dve", ua[0:64, 0:cn], ua[0:64, 0:cn], 3.141592, -3.141592, ALU.min, ALU.max, [ua], [ua])
                    k.act(dst[0:64, c0:c0 + cn], ua[0:64, 0:cn], AF.Sin, [ua], [dst])
            dec = P.alloc([L], F32)
            kf = [P.alloc([L], F32) for _ in range(2)]
            nr = P.alloc([4], F32)
            for g in range(NG):
                k.act(dec[:], trow[:], AF.Exp, [trow, ndl], [dec], scale=ndl[:, g:g + 1])
                for order in range(2):
                    for dr in range(2):
                        c00 = dr * 2048 + order * 1024 + g * 128
                        for (c0, cn) in tok_groups(L):
                            bank = nb()
                            k.mm(bank[:, 0:cn], w3[0:64, c00:c00 + 128], h2[0:64, c0:c0 + cn], True, True, [w3, h2], [bank])
                            k.tt("dve", kf[dr][:, c0:c0 + cn], bank[:, 0:cn], dec[:, c0:c0 + cn], ALU.mult, [bank, dec], [kf[dr]])
                    k.memset("dve", kf[1][:, 0:1], 0.0, [kf[1]])
                    for dr in range(2):
                        P.op("dve", lambda e, o=nr[:, dr:dr + 1], i=kf[dr][:]: e.tensor_reduce(
                            out=o, in_=i, axis=AX.X, op=ALU.add, apply_absolute_value=True), [kf[dr]], [nr])
                    k.stt(nr[:, 2:3], nr[:, 0:1], 1e-6, nr[:, 1:2], ALU.add, ALU.add, [nr], [nr])
                    k.recip(nr[:, 3:4], nr[:, 2:3], [nr], [nr])
                    for dr in range(2):
                        k.ts("dve", kf[dr][:], kf[dr][:], nr[:, 3:4], None, ALU.mult, None, [kf[dr], nr], [kf[dr]])
                        P.dma("sp", hyK[tag][dr * 2 + order, g * 128:(g + 1) * 128, :], kf[dr][:], r=[kf[dr]], w=[R_hyK])
            P.release(m1)
        NB = 16
        for (L, r0, tag) in seqs:
            if last and tag == "c":
                continue
            m1 = P.mark()
            tb = tabs[tag]
            N1 = 2 * L // 128
            H1 = N1 // 2
            P2 = 2 * N1
            DA = P.alloc([4 * N1], F32)
            TW1 = P.alloc([3, N1], F32)
            DB = P.alloc([3, 128], F32)
            IC = P.alloc([2, 256], F32)
            TW2 = P.alloc([3, 128], F32)
            DD = P.alloc([2, N1], F32)
            P.dma("sp", DA[0:N1, :], tb["DA"][:, :], w=[DA])
            P.dma("sp", TW1[:], tb["TW1"][:, :, :], w=[TW1])
            P.dma("sp", DB[:], tb["DB"][:, :, :], w=[DB])
            P.dma("sp", IC[:], tb["IC"][:, :, :], w=[IC])
            P.dma("sp", TW2[0:P2, :, :], tb["TW2"][:, :, :], w=[TW2])
            P.dma("sp", DD[0:P2, :, :], tb["DD"][:, :, :], w=[DD])
            Kt = P.alloc([128, 2, N1], F32)
            zg = P.alloc([64, 128], F32)
            Ap = P.alloc([NB, 2, N1], F32)
            Yt = P.alloc([2, NB, N1], F32)
            Ytf = Yt[:].rearrange("p r q n -> p r (q n)")
            Bp = P.alloc([NB // 2, 2, 128], F32)
            ct1 = P.alloc([4 * 2 * max(N1, 64)], F32)
            pq = [P.alloc([8, N1], F32) for _ in range(4)]
            Xa = [P.alloc([NB // 2, 128], F32) for _ in range(2)]
            Xb = [P.alloc([NB // 2, 128], F32) for _ in range(2)]
            dT = P.alloc([NB // 2], F32)
            tt_ = P.alloc([4, 128], F32)
            ob = P.alloc([NB // 2, 128], BF16)

            def load_T(dst, src2d, c0, rres):
                for c in range(2):
                    P.dma("sp", dst[c * H1:(c + 1) * H1, :, :],
                          src2d[c0 + c:c0 + NB:2, :].rearrange("q (a b) -> a q b", b=128), r=[rres], w=[dst])

            def fft_fwd(X, XR, consume):
                for pb in range(NB // 4):
                    bank = nb()
                    for pp in range(2):
                        k.mm(bank[:, pp * 4 * N1:(pp + 1) * 4 * N1], X[0:N1, pb * 2 + pp, :], DA[0:N1, :], True, True, [XR, DA], [bank])
                    A = bank[:, 0:8 * N1].rearrange("p (q r n) -> p q r n", q=4, r=2)
                    cmul(A, Ap[:, pb * 4:(pb + 1) * 4, :, :], ct1[:, 0:8 * N1].rearrange("p (q r n) -> p q r n", q=4, r=2),
                         TW1[:, 0, :], TW1[:, 1, :], TW1[:, 2, :], [bank], [TW1], [Ap], [ct1])
                for cg in range(NB // 8):
                    bre, bim = nb(), nb()
                    cs_ = slice(cg * 8, (cg + 1) * 8)
                    ore = bre[:, 0:8 * N1].rearrange("p (q n) -> p q n", q=8)
                    oim = bim[:, 0:8 * N1].rearrange("p (q n) -> p q n", q=8)
                    fre, fim = bre[:, 0:8 * N1], bim[:, 0:8 * N1]
                    k.mm(fre, DB[:, 0, :], Ap[:, cs_, 0, :], True, False, [DB, Ap], [bre])
                    k.mm(fre, DB[:, 1, :], Ap[:, cs_, 1, :], False, True, [DB, Ap], [bre])
                    k.mm(fim, DB[:, 0, :], Ap[:, cs_, 1, :], True, False, [DB, Ap], [bim])
                    k.mm(fim, DB[:, 2, :], Ap[:, cs_, 0, :], False, True, [DB, Ap], [bim])
                    consume(cg, bre, bim, ore, oim)

            def conv(X, XR, ch0, epilogue):
                def consume(cg, bre, bim, ore, oim):
                    kc_ = slice(ch0 + cg * 8, ch0 + (cg + 1) * 8)
                    ys = slice(cg * 8, (cg + 1) * 8)
                    k.tt("dve", pq[0][:], ore, Kt[:, kc_, 0, :], ALU.mult, [bre, Kt], [pq[0]])
                    k.tt("dve", pq[1][:], oim, Kt[:, kc_, 1, :], ALU.mult, [bim, Kt], [pq[1]])
                    k.tt("pool", Yt[:, 0, ys, :], pq[0][:], pq[1][:], ALU.subtract, [pq[0], pq[1]], [Yt])
                    k.tt("dve", pq[2][:], ore, Kt[:, kc_, 1, :], ALU.mult, [bre, Kt], [pq[2]])
                    k.tt("dve", pq[3][:], oim, Kt[:, kc_, 0, :], ALU.mult, [bim, Kt], [pq[3]])
                    k.tt("pool", Yt[:, 1, ys, :], pq[2][:], pq[3][:], ALU.add, [pq[2], pq[3]], [Yt])
                fft_fwd(X, XR, consume)
                for pb in range(NB // 4):
                    bank = nb()
                    for pp in range(2):
                        pr = pb * 2 + pp
                        o_ = bank[0:P2, pp * 256:(pp + 1) * 256]
                        k.mm(o_, Ytf[:, 0, 2 * pr * N1:(2 * pr + 2) * N1], IC[:, 0, :], True, False, [Yt, IC], [bank])
                        k.mm(o_, Ytf[:, 1, 2 * pr * N1:(2 * pr + 2) * N1], IC[:, 1, :], False, True, [Yt, IC], [bank])
                    A = bank[0:P2, :].rearrange("p (q r n) -> p q r n", q=2, r=2)
                    cmul(A, Bp[0:P2, pb * 2:(pb + 1) * 2, :, :], ct1[0:P2, 0:512].rearrange("p (q r n) -> p q r n", q=2, r=2),
                         TW2[0:P2, 0, :], TW2[0:P2, 1, :], TW2[0:P2, 2, :], [bank], [TW2], [Bp], [ct1])
                for pg in range(NB // 8):
                    bank = nb()
                    ps_ = slice(pg * 4, (pg + 1) * 4)
                    o_ = bank[0:N1, :].rearrange("p (q n) -> p q n", q=4)
                    k.mm(bank[0:N1, :], DD[0:P2, 0, :], Bp[0:P2, ps_, 0, :], True, False, [DD, Bp], [bank])
                    k.mm(bank[0:N1, :], DD[0:P2, 1, :], Bp[0:P2, ps_, 1, :], False, True, [DD, Bp], [bank])
                    epilogue(pg, bank, o_)

            def load_d(order, ch0):
                for c in range(2):
                    P.dma("sp", dT[c * H1:(c + 1) * H1, :], hy_d[l, order, c, ch0 // 2:ch0 // 2 + NB // 2].partition_broadcast(H1), w=[dT])

            for g in range(NG):
                for order in range(2):
                    for dr in range(2):
                        for sb in range(128 // NB):
                            X = Xa[sb % 2]
                            load_T(X, hyK[tag][dr * 2 + order], g * 128 + sb * NB, R_hyK)

                            def consume(cg, bre, bim, ore, oim, dr=dr, sb=sb):
                                kc_ = slice(sb * NB + cg * 8, sb * NB + (cg + 1) * 8)
                                if dr == 0:
                                    k.copy("act", Kt[:, kc_, 0, :], ore, [bre], [Kt])
                                    k.copy("dve", Kt[:, kc_, 1, :], oim, [bim], [Kt])
                                else:
                                    k.tt("dve", Kt[:, kc_, 0, :], Kt[:, kc_, 0, :], ore, ALU.add, [Kt, bre], [Kt])
                                    k.tt("dve", Kt[:, kc_, 1, :], Kt[:, kc_, 1, :], oim, ALU.subtract, [Kt, bim], [Kt])
                            fft_fwd(X, X, consume)
                    for sb in range(128 // NB):
                        ch0 = g * 128 + sb * NB
                        zs = zg[0:N1, sb * (NB // 2):(sb + 1) * (NB // 2), :]
                        bcd = lambda: dT[0:N1, :].unsqueeze(2).to_broadcast([N1, NB // 2, 128])
                        load_d(order, ch0)
                        if order == 0:
                            Xv, Xx = Xa[sb % 2], Xb[sb % 2]
                            load_T(Xv, hyS[0][:, r0:r0 + L], ch0, R_hyS)
                            load_T(Xx, hyS[1][:, r0:r0 + L], ch0, R_hyS)

                            def epi(pg, bank, o_, Xv=Xv, Xx=Xx, zs=zs):
                                ps_ = slice(pg * 4, (pg + 1) * 4)
                                dv = dT[0:N1, ps_].unsqueeze(2).to_broadcast([N1, 4, 128])
                                k.tt("pool", tt_[0:N1], Xv[0:N1, ps_, :], dv, ALU.mult, [Xv, dT], [tt_])
                                k.tt("dve", tt_[0:N1], tt_[0:N1], o_, ALU.add, [tt_, bank], [tt_])
                                k.tt("pool", zs[:, ps_, :], tt_[0:N1], Xx[0:N1, ps_, :], ALU.mult, [tt_, Xx], [zg])
                            conv(Xv, Xv, sb * NB, epi)
                        else:
                            Xx, Xg = Xa[sb % 2], Xb[sb % 2]
                            load_T(Xx, hyS[2][:, r0:r0 + L], ch0, R_hyS)
                            load_T(Xg, pFM[FM_HG:FM_HG + 1024, r0:r0 + L], ch0, R_pFM)
                            k.act(Xg[0:N1], Xg[0:N1], AF.Silu, [Xg], [Xg])

                            def epi(pg, bank, o_, Xx=Xx, Xg=Xg, zs=zs):
                                ps_ = slice(pg * 4, (pg + 1) * 4)
                                dv = dT[0:N1, ps_].unsqueeze(2).to_broadcast([N1, 4, 128])
                                k.tt("pool", tt_[0:N1], zs[:, ps_, :], dv, ALU.mult, [zg, dT], [tt_])
                                k.tt("dve", tt_[0:N1], tt_[0:N1], o_, ALU.add, [tt_, bank], [tt_])
                                k.tt("pool", tt_[0:N1], tt_[0:N1], Xx[0:N1, ps_, :], ALU.mult, [tt_, Xx], [tt_])
                                k.tt("pool", ob[0:N1, ps_, :], tt_[0:N1], Xg[0:N1, ps_, :], ALU.mult, [tt_, Xg], [ob])
                            conv(zs, zg, sb * NB, epi)
                            for c in range(2):
                                P.dma("sp", yT[1024 + ch0 + c:1024 + ch0 + NB:2, r0:r0 + L].rearrange("q (a b) -> a q b", b=128),
                                      ob[c * H1:(c + 1) * H1, :, :], r=[ob], w=[R_yT])
            P.release(m1)
        P.release(m0)

    return run


def make_deltanet(env):
    P, k, cfg = env["P"], env["k"], env["cfg"]
    T, NT, NCT, T_CTX, T_LAT = env["T"], env["NT"], env["NCT"], env["T_CTX"], env["T_LAT"]
    pTM, pFM, yT = env["pTM"], env["pFM"], env["yT"]
    R_pTM, R_pFM, R_yT = env["R_pTM"], env["R_pFM"], env["R_yT"]
    identf, identb, evac, inp = env["identf"], env["identb"], env["evac"], env["inp"]
    WD = cfg.get("wdepth", DEPTH)
    NJ = cfg.get("dn_heads", 4)
    dn_cw = inp("dn_cw", [WD, 128, 3, 16])
    dn_alog = inp("dn_a_log", [WD, 16])
    dn_dtb = inp("dn_dt_bias", [WD, 16])
    dn_ng = inp("dn_norm_g", [WD, 128])
    dn_masks = inp("dn_masks", [128, 5, 128])
    dnS = P.dram("dnS", [16, 128, T], F32, "Internal")
    R_dnS = Res()
    rb = {"i": 0}

    def nb():
        b = P.banks[rb["i"] % 8]
        rb["i"] += 1
        return b

    def run(l, last):
        m0 = P.mark()
        cw = P.alloc([3, 16], F32)
        ones = P.alloc([128], F32)
        P.dma("sp", cw[:], dn_cw[l], w=[cw])
        k.memset("dve", ones[:], 1.0, [ones])
        xi2 = [P.alloc([T], F32) for _ in range(2)]
        xo2 = [P.alloc([T], F32) for _ in range(2)]
        sq = P.alloc([512], F32)
        rt = P.alloc([512], F32)
        segs = [(0, T_CTX), (T_CTX, T)]
        for grp in range(16):
            if (grp < 8 and grp % 4 >= NJ) or (grp >= 8 and (grp - 8) // 2 >= NJ):
                continue
            xi, xo = xi2[grp % 2], xo2[grp % 2]
            P.dma("sp", xi[:], pFM[FM_DQ + grp * 128:FM_DQ + (grp + 1) * 128, :], r=[R_pFM], w=[xi])
            k.ts("dve", xo[:], xi[:], cw[:, 1, grp:grp + 1], None, ALU.mult, None, [xi, cw], [xo])
            for (a, b) in segs:
                k.stt(xo[:, a + 1:b], xi[:, a:b - 1], cw[:, 0, grp:grp + 1], xo[:, a + 1:b], ALU.mult, ALU.add, [xi, cw, xo], [xo])
                k.stt(xo[:, a:b - 1], xi[:, a + 1:b], cw[:, 2, grp:grp + 1], xo[:, a:b - 1], ALU.mult, ALU.add, [xi, cw, xo], [xo])
            k.act(xo[:], xo[:], AF.Silu, [xo], [xo])
            if grp < 8:
                for (c0, cn) in tok_groups(T):
                    bank = nb()
                    k.tt("pool", sq[:, 0:cn], xo[:, c0:c0 + cn], xo[:, c0:c0 + cn], ALU.mult, [xo], [sq])
                    k.mm(bank[:, 0:cn], ones[:], sq[:, 0:cn], True, True, [ones, sq], [bank])
                    k.ts("dve", rt[:, 0:cn], bank[:, 0:cn], 1e-6, None, ALU.add, None, [bank], [rt])
                    k.act(rt[:, 0:cn], rt[:, 0:cn], AF.Sqrt, [rt], [rt])
                    k.recip(rt[:, 0:cn], rt[:, 0:cn], [rt], [rt])
                    if grp < 4:
                        k.stt(xo[:, c0:c0 + cn], xo[:, c0:c0 + cn], 128.0 ** -0.5, rt[:, 0:cn], ALU.mult, ALU.mult, [xo, rt], [xo])
                    else:
                        k.tt("dve", xo[:, c0:c0 + cn], xo[:, c0:c0 + cn], rt[:, 0:cn], ALU.mult, [xo, rt], [xo])
            P.dma("sp", dnS[grp], xo[:], r=[xo], w=[R_dnS])
        P.release(m0)
        if cfg.get("dn_stop", 9) <= 1:
            return
        masks = P.alloc([5, 128], F32)
        ones = P.alloc([128], F32)
        ngb = P.alloc([128], F32)
        alb = P.alloc([16], F32)
        dtb = P.alloc([16], F32)
        P.dma("sp", masks[:], dn_masks[:, :, :], w=[masks])
        P.dma("sp", ngb[:], dn_ng[l].partition_broadcast(128), w=[ngb])
        P.dma("sp", alb[:], dn_alog[l].partition_broadcast(128), w=[alb])
        P.dma("sp", dtb[:], dn_dtb[l].partition_broadcast(128), w=[dtb])
        k.memset("dve", ones[:], 1.0, [ones])
        k.act(alb[:], alb[:], AF.Exp, [alb], [alb])
        k.ts("dve", alb[:], alb[:], -1.0, None, ALU.mult, None, [alb], [alb])
        ab = P.alloc([NT, 32], F32)
        P.dma("sp", ab[:], pTM[:, TM_A:TM_A + 32].rearrange("(n p) c -> p n c", p=128), r=[R_pTM], w=[ab])
        gall = P.alloc([NT, 16], F32)
        ball = P.alloc([NT, 16], F32)
        k.tt("dve", gall[:], ab[:, :, 0:16], dtb[:].unsqueeze(1).to_broadcast([128, NT, 16]), ALU.add, [ab, dtb], [gall])
        k.act(gall[:], gall[:], AF.Exp, [gall], [gall])
        k.act(gall[:], gall[:], AF.Ln, [gall], [gall], bias=1.0)
        k.tt("dve", gall[:], gall[:], alb[:].unsqueeze(1).to_broadcast([128, NT, 16]), ALU.mult, [gall, alb], [gall])
        k.act(ball[:], ab[:, :, 16:32], AF.Sigmoid, [ab], [ball])
        m1 = P.mark()
        if cfg.get("dn_stop", 9) <= 2:
            return
        for j in range(NJ):
            m1 = P.mark()
            qT = P.alloc([T], F32)
            kT = P.alloc([T], F32)
            vT = [P.alloc([T], F32) for _ in range(2)]
            P.dma("sp", qT[:], dnS[j], r=[R_dnS], w=[qT])
            P.dma("sp", kT[:], dnS[4 + j], r=[R_dnS], w=[kT])
            for hh in range(2):
                P.dma("sp", vT[hh][:], dnS[8 + 2 * j + hh], r=[R_dnS], w=[vT[hh]])
            oacc = P.alloc([NT, 2, 128], F32, nres=NT)
            k.memset("pool", oacc[:], 0.0, oacc.res)
            sq_ = lambda: P.alloc([128], F32)
            shared = [dict(ktok=sq_(), kk=sq_(), qkT=sq_()) for _ in range(2)]
            def mk_slot(bk_, q_):
                t_ = Tile(P.banks[bk_].ap[:, q_ * 128:(q_ + 1) * 128])
                t_.r = P.banks[bk_].r
                t_.res = [t_.r]
                return t_
            slots = [[mk_slot(2 * c4 + bb, q_) for q_ in range(4) for bb in range(2)] for c4 in range(4)]
            srot = [0, 0, 0, 0]

            def nbs(c4):
                t_ = slots[c4][srot[c4] % 8]
                srot[c4] += 1
                return t_
            hd = []
            for hh in range(4):
                hd.append(dict(S=sq_(), vtok=sq_(), grep=sq_(), brep=sq_(), gcr=sq_(), e1=sq_(), DT=sq_(), QKT=sq_(), tM=sq_(),
                               M=sq_(), Rm=sq_(), Pa=sq_(), Pb=sq_(), Qa=sq_(), Qb=sq_(), Er=sq_(), qdT=sq_(), vb=sq_(), kbe=sq_(),
                               kdec=sq_(), u=sq_(), wT=sq_(), vnew=sq_(), cols=P.alloc([8], F32)))

            def chain(hh, dr, n):
                d = hd[dr * 2 + hh]
                nb = lambda: nbs(dr * 2 + hh)
                ktok, kk, qkT = shared[dr]["ktok"], shared[dr]["kk"], shared[dr]["qkT"]
                h = 2 * j + hh
                ci = dr * 8 + h
                gcol = gall[:, n, ci:ci + 1]
                bcol = ball[:, n, ci:ci + 1]
                Mk, Ms, Id = masks[:, 2 * dr, :], masks[:, 2 * dr + 1, :], masks[:, 4, :]
                cs_ = slice(n * 128, (n + 1) * 128)
                cols = d["cols"]
                k.ts("pool", d["grep"][:], ones[:], gcol, None, ALU.mult, None, [ones, gall], [d["grep"]])
                k.ts("pool", d["brep"][:], ones[:], bcol, None, ALU.mult, None, [ones, ball], [d["brep"]])
                b1, b2, b3 = nb(), nb(), nb()
                k.mm(b1[:, 0:128], d["grep"][:], Mk, True, True, [d["grep"], masks], [b1])
                k.mm(b2[:, 0:1], Mk, gcol, True, True, [masks, gall], [b2])
                k.mm(b3[:, 0:128], d["brep"][:], Id, True, True, [d["brep"], masks], [b3])
                yield
                k.copy("dve", cols[:, 0:1], b2[:, 0:1], [b2], [cols])
                k.copy("act", d["gcr"][:], b1[:, 0:128], [b1], [d["gcr"]])
                gcl = d["gcr"][:, 127:128] if dr == 0 else d["gcr"][:, 0:1]
                k.ts("dve", d["e1"][:], d["gcr"][:], cols[:, 0:1], 0.0, ALU.subtract, ALU.min, [d["gcr"], cols], [d["e1"]])
                k.act(d["e1"][:], d["e1"][:], AF.Exp, [d["e1"]], [d["e1"]])
                k.tt("pool", d["DT"][:], d["e1"][:], Mk, ALU.mult, [d["e1"], masks], [d["DT"]])
                k.tt("pool", d["QKT"][:], qkT[:], d["DT"][:], ALU.mult, [qkT, d["DT"]], [d["QKT"]])
                k.tt("pool", d["tM"][:], d["DT"][:], Ms, ALU.mult, [d["DT"], masks], [d["tM"]])
                k.tt("pool", d["tM"][:], d["tM"][:], kk[:], ALU.mult, [d["tM"], kk], [d["tM"]])
                k.tt("dve", d["M"][:], d["tM"][:], b3[:, 0:128], ALU.mult, [d["tM"], b3], [d["M"]])
                k.act(cols[:, 1:2], cols[:, 0:1], AF.Exp, [cols], [cols])
                k.ts("dve", cols[:, 2:3], gcl, cols[:, 0:1], None, ALU.subtract, None, [d["gcr"], cols], [cols])
                k.act(cols[:, 2:3], cols[:, 2:3], AF.Exp, [cols], [cols])
                k.act(cols[:, 3:4], gcl, AF.Exp, [d["gcr"]], [cols])
                k.act(d["Er"][:], d["gcr"][:], AF.Exp, [d["gcr"]], [d["Er"]])
                k.tt("pool", d["qdT"][:], qT[:, cs_], d["Er"][:], ALU.mult, [qT, d["Er"]], [d["qdT"]])
                k.ts("dve", d["vb"][:], d["vtok"][:], bcol, None, ALU.mult, None, [d["vtok"], ball], [d["vb"]])
                k.ts("dve", d["kbe"][:], ktok[:], bcol, cols[:, 1:2], ALU.mult, ALU.mult, [ktok, ball, cols], [d["kbe"]])
                k.ts("dve", d["kdec"][:], ktok[:], cols[:, 2:3], None, ALU.mult, None, [ktok, cols], [d["kdec"]])
                k.tt("pool", d["Rm"][:], Id, d["M"][:], ALU.subtract, [masks, d["M"]], [d["Rm"]])
                bq = nb()
                k.tr(bq[:, 0:128], d["M"][:], identf[:], [d["M"], identf], [bq])
                yield
                Pc, Qc, Pn, Qn = d["M"], d["Qa"], d["Pa"], d["Qb"]
                k.copy("act", Qc[:], bq[:, 0:128], [bq], [Qc])
                for lev in range(1, 7):
                    bq2 = nb()
                    k.mm(bq2[:, 0:128], Pc[:], Qc[:], True, True, [Pc, Qc], [bq2])
                    if lev < 6:
                        bp2 = nb()
                        k.mm(bp2[:, 0:128], Qc[:], Pc[:], True, True, [Pc, Qc], [bp2])
                    yield
                    k.copy("act", Qn[:], bq2[:, 0:128], [bq2], [Qn])
                    if lev < 6:
                        k.copy("dve", Pn[:], bp2[:, 0:128], [bp2], [Pn])
                    br_ = nb()
                    k.mm(br_[:, 0:128], Qn[:], d["Rm"][:], True, True, [Qn, d["Rm"]], [br_])
                    yield
                    k.tt("dve", d["Rm"][:], d["Rm"][:], br_[:, 0:128], ALU.add, [d["Rm"], br_], [d["Rm"]])
                    Pc, Qc = Pn, Qn
                    Pn = d["Pb"] if Pc is d["Pa"] else d["Pa"]
                    Qn = d["Qa"] if Qc is d["Qb"] else d["Qb"]
                bu, bw = nb(), nb()
                k.mm(bu[:, 0:128], d["Rm"][:], d["vb"][:], True, True, [d["Rm"], d["vb"]], [bu])
                k.mm(bw[:, 0:128], d["kbe"][:], d["Rm"][:], True, True, [d["Rm"], d["kbe"]], [bw])
                yield
                k.copy("act", d["u"][:], bu[:, 0:128], [bu], [d["u"]])
                k.copy("dve", d["wT"][:], bw[:, 0:128], [bw], [d["wT"]])
                bv = nb()
                k.mm(bv[:, 0:128], d["wT"][:], d["S"][:], True, True, [d["wT"], d["S"]], [bv])
                yield
                k.tt("dve", d["vnew"][:], d["u"][:], bv[:, 0:128], ALU.subtract, [d["u"], bv], [d["vnew"]])
                bo, bs = nb(), nb()
                k.mm(bo[:, 0:128], d["qdT"][:], d["S"][:], True, False, [d["qdT"], d["S"]], [bo])
                k.mm(bo[:, 0:128], d["QKT"][:], d["vnew"][:], False, True, [d["QKT"], d["vnew"]], [bo])
                k.mm(bs[:, 0:128], d["kdec"][:], d["vnew"][:], True, True, [d["kdec"], d["vnew"]], [bs])
                yield
                k.tt("dve", oacc[:, n, hh, :], oacc[:, n, hh, :], bo[:, 0:128], ALU.add, [oacc.res[n], bo], [oacc.res[n]])
                k.stt(d["S"][:], d["S"][:], cols[:, 3:4], bs[:, 0:128], ALU.mult, ALU.add, [d["S"], cols, bs], [d["S"]])

            orders = [list(range(NT)), list(range(NCT - 1, -1, -1)) + list(range(NT - 1, NCT - 1, -1))]
            for c4 in range(4):
                k.memset("pool", hd[c4]["S"][:], 0.0, [hd[c4]["S"]])
            for i_ in range(NT):
                gens = []
                for dr in range(2):
                    n = orders[dr][i_]
                    ktok, kk, qkT = shared[dr]["ktok"], shared[dr]["kk"], shared[dr]["qkT"]
                    cs_ = slice(n * 128, (n + 1) * 128)
                    bt = nbs(dr * 2)
                    k.tr(bt[:, 0:128], kT[:, cs_], identf[:], [kT, identf], [bt])
                    k.copy("act", ktok[:], bt[:, 0:128], [bt], [ktok])
                    for hh in range(2):
                        bt2 = nbs(dr * 2 + 1)
                        k.tr(bt2[:, 0:128], vT[hh][:, cs_], identf[:], [vT[hh], identf], [bt2])
                        k.copy("dve", hd[dr * 2 + hh]["vtok"][:], bt2[:, 0:128], [bt2], [hd[dr * 2 + hh]["vtok"]])
                    bk, bqk = nbs(dr * 2), nbs(dr * 2 + 1)
                    k.mm(bk[:, 0:128], kT[:, cs_], kT[:, cs_], True, True, [kT], [bk])
                    k.mm(bqk[:, 0:128], kT[:, cs_], qT[:, cs_], True, True, [kT, qT], [bqk])
                    k.copy("act", kk[:], bk[:, 0:128], [bk], [kk])
                    k.copy("dve", qkT[:], bqk[:, 0:128], [bqk], [qkT])
                    gens += [chain(hh, dr, n) for hh in range(2)]
                alive = [True] * 4
                nseg = 0
                while any(alive):
                    nseg += 1
                    if nseg > cfg.get("dn_segs", 999):
                        break
                    for c4 in range(4):
                        if alive[c4]:
                            try:
                                next(gens[c4])
                            except StopIteration:
                                alive[c4] = False
            zt2 = [P.alloc([256], F32) for _ in range(2)]
            yn = P.alloc([128], F32)
            yb = P.alloc([2, 128], BF16)
            yo2 = [P.alloc([2, 128], BF16) for _ in range(2)]
            st = P.alloc([2], F32)
            for t in range(NT):
                zt = zt2[t % 2]
                P.dma("sp", zt[:], pTM[t * 128:(t + 1) * 128, TM_Z + 2 * j * 128:TM_Z + (2 * j + 2) * 128], r=[R_pTM], w=[zt])
                k.act(zt[:], zt[:], AF.Silu, [zt], [zt])
                for hh in range(2):
                    k.ssq(yn[:], oacc[:, t, hh, :], st[:, 0:1], [oacc.res[t]], [yn, st])
                    k.ts("dve", st[:, 0:1], st[:, 0:1], 1.0 / 128, EPS, ALU.mult, ALU.add, [st], [st])
                    k.act(st[:, 0:1], st[:, 0:1], AF.Sqrt, [st], [st])
                    k.recip(st[:, 1:2], st[:, 0:1], [st], [st])
                    k.stt(yn[:], oacc[:, t, hh, :], st[:, 1:2], ngb[:], ALU.mult, ALU.mult, [oacc.res[t], st, ngb], [yn])
                    k.tt("pool", yb[:, hh, :], yn[:], zt[:, hh * 128:(hh + 1) * 128], ALU.mult, [yn, zt], [yb])
                bank = P.banks[t % 2]
                bv_ = bank_bf(P, t % 2).rearrange("p (a b) -> p a b", a=8)
                for hh in range(2):
                    k.tr(bv_[:, hh, :], yb[:, hh, :], identb[:], [yb, identb], [bank])
                yo = yo2[t % 2]
                evac(yo[:], bv_[:, 0:2, :], [bank], [yo])
                P.dma("sp", yT[2048 + 2 * j * 128:2048 + (2 * j + 2) * 128, t * 128:(t + 1) * 128].rearrange("(a p) t -> p a t", p=128),
                      yo[:], r=[yo], w=[R_yT])
            P.release(m1)
        P.release(m0)

    return run


def dn_mask_table():
    s = np.arange(128)[:, None]
    c = np.arange(128)[None, :]
    m = np.stack([s <= c, s < c, s >= c, s > c, s == c], axis=1).astype(np.float32)
    return np.ascontiguousarray(m)


def deltanet_inputs(w):
    f32 = np.float32
    cwf = np.asarray(w["dn_conv_w"], f32)
    WD = cwf.shape[0]
    cw = cwf.reshape(WD, 3, 16, 128).transpose(0, 3, 1, 2)
    return dict(dn_cw=np.ascontiguousarray(cw), dn_a_log=np.ascontiguousarray(np.asarray(w["dn_a_log"], f32).reshape(WD, 16)),
                dn_dt_bias=np.ascontiguousarray(np.asarray(w["dn_dt_bias"], f32).reshape(WD, 16)),
                dn_norm_g=np.ascontiguousarray(np.asarray(w["dn_norm_g"], f32)), dn_masks=dn_mask_table())


def hyena_inputs(w, t_ctx, t_lat):
    f32 = np.float32
    cwf = np.asarray(w["hy_conv_w"], f32)
    WD = cwf.shape[0]
    cw = cwf.reshape(WD, 3, 24, 128).transpose(0, 3, 1, 2)
    cb = np.asarray(w["hy_conv_b"], f32).reshape(WD, 24, 128).transpose(0, 2, 1)
    out = dict(hy_cw=np.ascontiguousarray(cw), hy_cb=np.ascontiguousarray(cb), hy_delta=hy_delta_table())
    out["hy_dp"] = np.ascontiguousarray(np.asarray(w["hy_d"], f32).reshape(WD, 2, 512, 2).transpose(0, 1, 3, 2))
    for n in ("hy_w1", "hy_b1", "hy_freq1", "hy_w2", "hy_b2", "hy_freq2", "hy_w3"):
        out[n] = np.ascontiguousarray(np.asarray(w[n], f32))
    for (L, tag) in ((t_lat, "l"), (t_ctx, "c")):
        tb = hy_tables(L)
        for k_, v in tb.items():
            out[f"hy_{k_}_{tag}"] = v
    return out


T_CTX_FULL, T_LAT_FULL = 256, 4096
_NC_CACHE = {}


def _get_nc():
    if "nc" not in _NC_CACHE:
        cfg = dict(t_ctx=T_CTX_FULL, t_lat=T_LAT_FULL, depth=DEPTH, half=17)
        _NC_CACHE["nc"] = build(cfg)
    return _NC_CACHE["nc"]


def kernel(**inputs):
    f32 = np.float32
    w = {k_: np.asarray(v) for k_, v in inputs.items()}
    B = w["x"].shape[0]
    shared = dict(
        norm_g=w["norm_g"], w_mod=w["w_mod"], b_mod=w["b_mod"], w_in=w["w_in"], w_pa=w["w_pa"], w_pb=w["w_pb"],
        w_pc=w["w_pc"], w_out=w["w_out"], final_g=w["final_g"], ident=np.eye(128, dtype=f32),
        q_norm_g=w["q_norm_g"], k_norm_g=w["k_norm_g"], rope=rope_table(T_LAT_FULL))
    shared.update(hyena_inputs(w, T_CTX_FULL, T_LAT_FULL))
    shared.update(deltanet_inputs(w))
    shared = {k_: np.ascontiguousarray(v, dtype=f32) for k_, v in shared.items()}
    in_maps = []
    for b in range(B):
        csT = np.stack([w["c"][b].reshape(NKC, 128).T, w["c_ctx"].reshape(NKC, 128).T], axis=-1)
        m = dict(shared)
        m["x"] = np.ascontiguousarray(w["x"][b], dtype=f32)
        m["ctx"] = np.ascontiguousarray(w["ctx"][b], dtype=f32)
        m["csT"] = np.ascontiguousarray(csT, dtype=f32)
        in_maps.append(m)
    nc = _get_nc()
    res = run_bass_kernel_spmd(nc, in_maps, core_ids=list(range(B)))
    return np.stack([np.asarray(r["out"], dtype=f32) for r in res.results], axis=0)
```
